# Optimizing a Trainium2 kernel written in Bass

```python
import jax, jax.numpy as jnp
from jax import lax
import numpy as np

D_MODEL = 1024
BATCH = 2
SEQ = 8192
DEPTH = 2

N_MEM = 256
D_MIX = D_MODEL
HGRN_DK = 128
HGRN_DV = 128
HGRN_WIDTH = D_MIX // 2
HGRN_HEADS = HGRN_WIDTH // HGRN_DV
HGRN_KW = HGRN_HEADS * HGRN_DK
FOX_DH = 64
FOX_WIDTH = D_MIX - HGRN_WIDTH
FOX_HEADS = FOX_WIDTH // FOX_DH
CHUNK = 64
Q_BLOCK = 128
D_FF = 2816
CROSS_HEADS = 4
CROSS_DH = D_MODEL // CROSS_HEADS
EPS = 1e-6
N_IN = 2 * HGRN_KW + 2 * HGRN_WIDTH + 3 * FOX_WIDTH + FOX_HEADS

kernel_name = "hymba_hgrn2_fox_macaron"


def rms_norm(x, w):
    x32 = x.astype(jnp.float32)
    y = x32 * lax.rsqrt(jnp.mean(x32 * x32, axis=-1, keepdims=True) + EPS)
    return (y * w.astype(jnp.float32)).astype(x.dtype)


def swiglu(h, w_gate, w_up, w_down):
    return (jax.nn.silu(h @ w_gate) * (h @ w_up)) @ w_down


def hgrn2_chunk_scan(q, k, v, logf):
    B, H, S, DK = q.shape
    DV = v.shape[-1]
    nc = S // CHUNK

    def to_chunks(t):
        return t.reshape(B, H, nc, CHUNK, t.shape[-1]).transpose(2, 0, 1, 3, 4)

    qc, kc, vc = to_chunks(q), to_chunks(k), to_chunks(v)
    gc = jnp.cumsum(to_chunks(logf), axis=-2)
    mask = jnp.tril(jnp.ones((CHUNK, CHUNK), dtype=bool))[None, None, :, :, None]

    def step(state, inp):
        q_, k_, v_, g_ = inp
        o_inter = jnp.einsum('bhtk,bhkv->bhtv', q_ * jnp.exp(g_), state)
        diff = g_[:, :, :, None, :] - g_[:, :, None, :, :]
        decay = jnp.exp(jnp.where(mask, diff, -jnp.inf))
        a = jnp.einsum('bhtk,bhsk,bhtsk->bhts', q_, k_, decay)
        o = o_inter + jnp.einsum('bhts,bhsv->bhtv', a, v_)
        g_last = g_[:, :, -1:, :]
        new_state = jnp.exp(g_last[:, :, 0, :])[..., None] * state + jnp.einsum(
            'bhsk,bhsv->bhkv', k_ * jnp.exp(g_last - g_), v_)
        return new_state, o

    s0 = jnp.zeros((B, H, DK, DV), jnp.float32)
    _, oc = lax.scan(step, s0, (qc, kc, vc, gc))
    return oc.transpose(1, 2, 0, 3, 4).reshape(B, H, S, DV)


def forgetting_attention(q, k, v, logf):
    B, H, S, dh = q.shape
    nb = S // Q_BLOCK
    c = jnp.cumsum(logf, axis=-1)
    qb = q.reshape(B, H, nb, Q_BLOCK, dh).transpose(2, 0, 1, 3, 4)
    cb = c.reshape(B, H, nb, Q_BLOCK).transpose(2, 0, 1, 3)
    kpos = jnp.arange(S)
    scale = dh ** -0.5

    def block(args):
        q_blk, c_blk, i = args
        s = jnp.einsum('bhqd,bhkd->bhqk', q_blk, k).astype(jnp.float32) * scale
        s = s + (c_blk[..., :, None] - c[:, :, None, :])
        qpos = i * Q_BLOCK + jnp.arange(Q_BLOCK)
        s = jnp.where(kpos[None, :] <= qpos[:, None], s, -jnp.inf)
        p = jax.nn.softmax(s, axis=-1)
        return jnp.einsum('bhqk,bhkd->bhqd', p.astype(v.dtype), v)

    ob = lax.map(block, (qb, cb, jnp.arange(nb)))
    return ob.transpose(1, 2, 0, 3, 4).reshape(B, H, S, dh)


def hybrid_mixer(h, w_in, lb, out_norm_w, fox_bf, w_out):
    B, S, _ = h.shape
    sizes = [HGRN_KW, HGRN_KW, HGRN_WIDTH, HGRN_WIDTH, FOX_WIDTH, FOX_WIDTH, FOX_WIDTH]
    offsets = [int(o) for o in np.cumsum(sizes)]
    z = h @ w_in
    hq, hf, hi, hg, fq, fk, fv, ff = jnp.split(z, offsets, axis=-1)

    def heads(t, n):
        return t.reshape(B, S, n, -1).transpose(0, 2, 1, 3)

    zf = hf.astype(jnp.float32)
    lb32 = lb.astype(jnp.float32)
    logf_h = jnp.logaddexp(jnp.log(lb32), jnp.log1p(-lb32) + jax.nn.log_sigmoid(zf))
    k_h = (1.0 - lb32) * jax.nn.sigmoid(-zf)
    q_h = jax.nn.silu(hq.astype(jnp.float32))
    o_h = hgrn2_chunk_scan(heads(q_h, HGRN_HEADS), heads(k_h, HGRN_HEADS),
                           heads(hi.astype(jnp.float32), HGRN_HEADS),
                           heads(logf_h, HGRN_HEADS))
    o_h = rms_norm(o_h.transpose(0, 2, 1, 3), out_norm_w).reshape(B, S, HGRN_WIDTH)
    o_h = o_h * jax.nn.silu(hg.astype(jnp.float32))

    logf_f = jax.nn.log_sigmoid(ff.astype(jnp.float32) + fox_bf.astype(jnp.float32))
    o_f = forgetting_attention(heads(fq, FOX_HEADS), heads(fk, FOX_HEADS),
                               heads(fv, FOX_HEADS), logf_f.transpose(0, 2, 1))
    o_f = o_f.transpose(0, 2, 1, 3).reshape(B, S, FOX_WIDTH)

    o = jnp.concatenate([o_h.astype(h.dtype), o_f.astype(h.dtype)], axis=-1)
    return o @ w_out


def memory_cross_attention(h, m, wq, wk, wv, wo):
    B, S, _ = h.shape
    M = m.shape[1]
    q = (h @ wq).reshape(B, S, CROSS_HEADS, CROSS_DH)
    k = (m @ wk).reshape(B, M, CROSS_HEADS, CROSS_DH)
    v = (m @ wv).reshape(B, M, CROSS_HEADS, CROSS_DH)
    s = jnp.einsum('bshd,bmhd->bhsm', q, k).astype(jnp.float32) * (CROSS_DH ** -0.5)
    p = jax.nn.softmax(s, axis=-1).astype(h.dtype)
    o = jnp.einsum('bhsm,bmhd->bshd', p, v).reshape(B, S, D_MODEL)
    return o @ wo


def setup_inputs(seed: int = 0) -> dict:
    key = jax.random.key(seed)
    ks = jax.random.split(key, 24)
    n = jax.random.normal
    f32 = jnp.float32

    def w(k, shape, fan_in):
        return n(k, shape, f32) * fan_in ** -0.5

    def gain(k, shape):
        return 1.0 + 0.05 * n(k, shape, f32)

    return {
        "x": n(ks[0], (BATCH, SEQ, D_MODEL), f32),
        "mem": n(ks[1], (BATCH, N_MEM, D_MODEL), f32),
        "ffn1_norm": gain(ks[2], (DEPTH, D_MODEL)),
        "ffn1_w_gate": w(ks[3], (DEPTH, D_MODEL, D_FF), D_MODEL),
        "ffn1_w_up": w(ks[4], (DEPTH, D_MODEL, D_FF), D_MODEL),
        "ffn1_w_down": w(ks[5], (DEPTH, D_FF, D_MODEL), D_FF),
        "mix_norm": gain(ks[6], (DEPTH, D_MODEL)),
        "w_in": w(ks[7], (DEPTH, D_MODEL, N_IN), D_MODEL),
        "hgrn_lb": n(ks[8], (DEPTH, HGRN_KW), f32),
        "hgrn_out_norm": gain(ks[9], (DEPTH, HGRN_DV)),
        "fox_f_bias": 2.0 + 0.5 * n(ks[10], (DEPTH, FOX_HEADS), f32),
        "w_out": w(ks[11], (DEPTH, D_MIX, D_MODEL), D_MIX),
        "cross_norm": gain(ks[12], (DEPTH, D_MODEL)),
        "mem_norm": gain(ks[13], (DEPTH, D_MODEL)),
        "cross_wq": w(ks[14], (DEPTH, D_MODEL, D_MODEL), D_MODEL),
        "cross_wk": w(ks[15], (DEPTH, D_MODEL, D_MODEL), D_MODEL),
        "cross_wv": w(ks[16], (DEPTH, D_MODEL, D_MODEL), D_MODEL),
        "cross_wo": w(ks[17], (DEPTH, D_MODEL, D_MODEL), D_MODEL),
        "ffn2_norm": gain(ks[18], (DEPTH, D_MODEL)),
        "ffn2_w_gate": w(ks[19], (DEPTH, D_MODEL, D_FF), D_MODEL),
        "ffn2_w_up": w(ks[20], (DEPTH, D_MODEL, D_FF), D_MODEL),
        "ffn2_w_down": w(ks[21], (DEPTH, D_FF, D_MODEL), D_FF),
        "final_norm": gain(ks[22], (D_MODEL,)),
    }


def reference(x, mem, ffn1_norm, ffn1_w_gate, ffn1_w_up, ffn1_w_down, mix_norm, w_in,
              hgrn_lb, hgrn_out_norm, fox_f_bias, w_out, cross_norm, mem_norm,
              cross_wq, cross_wk, cross_wv, cross_wo, ffn2_norm, ffn2_w_gate,
              ffn2_w_up, ffn2_w_down, final_norm):
    lb_cum = jnp.cumsum(jax.nn.softmax(hgrn_lb.astype(jnp.float32), axis=0), axis=0)
    lb_all = lb_cum - lb_cum[0:1]
    for l in range(DEPTH):
        x = x + 0.5 * swiglu(rms_norm(x, ffn1_norm[l]), ffn1_w_gate[l], ffn1_w_up[l], ffn1_w_down[l])
        x = x + hybrid_mixer(rms_norm(x, mix_norm[l]), w_in[l], lb_all[l], hgrn_out_norm[l],
                             fox_f_bias[l], w_out[l])
        x = x + memory_cross_attention(rms_norm(x, cross_norm[l]), rms_norm(mem, mem_norm[l]),
                                       cross_wq[l], cross_wk[l], cross_wv[l], cross_wo[l])
        x = x + 0.5 * swiglu(rms_norm(x, ffn2_norm[l]), ffn2_w_gate[l], ffn2_w_up[l], ffn2_w_down[l])
    return rms_norm(x, final_norm)
```

```python
import contextlib
import numpy as np
import ml_dtypes
import concourse.bass as bass
import concourse.mybir as mybir
from concourse.bass_utils import run_bass_kernel_spmd

F32 = mybir.dt.float32
BF16 = mybir.dt.bfloat16
ALU = mybir.AluOpType
AF = mybir.ActivationFunctionType
NPBF = ml_dtypes.bfloat16

NDMA = 32
EPS = 1e-6


class Op:
    __slots__ = ("eng", "fn", "deps", "sig", "sigval", "is_dma", "dsem", "dval", "extra")

    def __init__(self, eng, fn, is_dma=False):
        self.eng = eng
        self.fn = fn
        self.deps = set()
        self.sig = False
        self.sigval = 0
        self.is_dma = is_dma
        self.dsem = None
        self.dval = 0


class V:
    __slots__ = ("ap", "key")

    def __init__(self, ap, key):
        self.ap = ap
        self.key = key

    def __getitem__(self, idx):
        return V(self.ap[idx], self.key)

    def k(self, key):
        return V(self.ap, key)


class Prog:
    ENGS = ("pe", "act", "dve", "pool", "sp")

    def __init__(self):
        self.q = {e: [] for e in self.ENGS}
        self.lastw = {}
        self.readers = {}
        self.dma_rr = 0
        self.dma_rr2 = 0
        self.dma_rr3 = 0
        self.dma_last = [None] * NDMA
        self.dma_cnt = [0] * NDMA
        self.out_dmas = []
        self.ncoll = 0
        self.coll_last = []
        self.pending = {}
        self.jv = {}
        self.persist = set()

    def barrier(self):
        lasts = set()
        for e in self.ENGS:
            for o in reversed(self.q[e]):
                if not o.is_dma and o.fn is not None:
                    o.sig = True
                    lasts.add(o)
                    break
        for o in self.dma_last:
            if o is not None:
                lasts.add(o)
        self.pending = {e: set(lasts) for e in self.ENGS}
        pk = lambda k: (k if isinstance(k, str) else k[0]) in self.persist
        self.lastw = {k: v for k, v in self.lastw.items() if pk(k)}
        self.readers = {k: v for k, v in self.readers.items() if pk(k)}

    def op(self, eng, fn, reads=(), writes=(), dma=False, final=False, coll=False, cast=False):
        o = Op(eng, fn, dma or coll)
        o.extra = 16
        deps = set(self.pending.pop(eng, ()))
        for r in reads:
            w = self.lastw.get(r)
            if w is not None:
                deps.add(w)
        for wk in writes:
            w = self.lastw.get(wk)
            if w is not None:
                deps.add(w)
            rd = self.readers.get(wk)
            if rd:
                deps.update(rd[0].values())
                deps.update(rd[1])
        if dma:
            if eng == "sp":
                k = self.dma_rr
                self.dma_rr = (k + 1) % 12
            elif cast:
                k = 24 + self.dma_rr3
                self.dma_rr3 = (self.dma_rr3 + 1) % 4
            else:
                k = 12 + self.dma_rr2
                self.dma_rr2 = (self.dma_rr2 + 1) % 12
            if self.dma_last[k] is not None:
                deps.add(self.dma_last[k])
            self.dma_cnt[k] += 1
            o.dsem = k
            o.dval = 16 * self.dma_cnt[k]
            self.dma_last[k] = o
        if coll:
            if self.coll_last:
                deps.add(self.coll_last[-1])
            o.dsem = NDMA
            self.ncoll += 1
            o.dval = self.ncoll
            o.extra = 1
            self.coll_last.append(o)
        for r in reads:
            rd = self.readers.setdefault(r, ({}, []))
            if dma or coll:
                rd[1].append(o)
            else:
                rd[0][eng] = o
        for wk in writes:
            self.lastw[wk] = o
            self.readers[wk] = ({}, [])
        deps.discard(o)
        if eng == "pe" and not dma:
            deps = {d for d in deps if d.is_dma or d.eng != "pe"}
        for d in deps:
            if not d.is_dma:
                d.sig = True
        o.deps = deps
        self.q[eng].append(o)
        if final:
            self.out_dmas.append(o)
        return o

    def emit(self, nc, es):
        esem = {e: es.enter_context(nc.semaphore("s_" + e)) for e in self.ENGS}
        dsem = [es.enter_context(nc.semaphore("d%d" % i)) for i in range(NDMA + 1)]
        for e in self.ENGS:
            c = 0
            for o in self.q[e]:
                if o.sig and not o.is_dma:
                    c += 1
                    o.sigval = c
        fin = Op("sp", None)
        fin.deps = set(self.out_dmas)
        self.q["sp"].append(fin)
        block = es.enter_context(nc.Block())
        prog = self

        def run(e, E):
            waited = {}
            if e == "sp":
                prog.jv["j"] = E.partition_id() % 4
            for o in prog.q[e]:
                need = {}
                for d in o.deps:
                    if d.is_dma:
                        s, v = ("d", d.dsem), d.dval
                    else:
                        s, v = ("e", d.eng), d.sigval
                    if need.get(s, 0) < v:
                        need[s] = v
                for s, v in need.items():
                    if waited.get(s, 0) >= v:
                        continue
                    waited[s] = v
                    E.wait_ge(dsem[s[1]] if s[0] == "d" else esem[s[1]], v)
                if o.fn is None:
                    continue
                ins = o.fn(E)
                if o.is_dma:
                    ins.then_inc(dsem[o.dsem], o.extra)
                elif o.sig:
                    ins.then_inc(esem[e], 1)

        @block.tensor
        def _(E):
            run("pe", E)

        @block.scalar
        def _(E):
            run("act", E)

        @block.vector
        def _(E):
            run("dve", E)

        @block.gpsimd
        def _(E):
            run("pool", E)

        @block.sync
        def _(E):
            run("sp", E)


class Ctx:
    def __init__(self):
        self.nc = bass.Bass("TRN2", target_bir_lowering=False)
        self.P = Prog()
        self.es = contextlib.ExitStack()
        self.nkey = 0

    def sb(self, name, shape, dt, key=None):
        t = self.es.enter_context(self.nc.sbuf_tensor(name, list(shape), dt))
        return V(t[:], key or name)

    def sbs(self, name, shape, dt, n):
        return [self.sb("%s%d" % (name, i), shape, dt) for i in range(n)]

    def ps(self, name, shape, dt):
        t = self.es.enter_context(self.nc.psum_tensor(name, list(shape), dt))
        return V(t[:], name)

    def dram(self, name, shape, dt, kind="Internal"):
        t = self.nc.dram_tensor(name, list(shape), dt, kind=kind)
        return V(t.ap(), name)

    def mm(self, out, lhsT, rhs, start=True, stop=True):
        r = [lhsT.key, rhs.key] + ([] if start else [out.key])
        self.P.op("pe", lambda E: E.matmul(out.ap, lhsT.ap, rhs.ap, start=start, stop=stop), r, [out.key])

    def transpose(self, out, in_, ident):
        self.P.op("pe", lambda E: E.transpose(out.ap, in_.ap, ident.ap), [in_.key, ident.key], [out.key])

    def act(self, out, in_, func, bias=None, scale=None):
        r = [in_.key]
        kw = {}
        if bias is not None:
            if isinstance(bias, V):
                r.append(bias.key)
                kw["bias"] = bias.ap
            else:
                kw["bias"] = float(bias)
        if scale is not None:
            kw["scale"] = float(scale)
        self.P.op("act", lambda E: E.activation(out=out.ap, in_=in_.ap, func=func, **kw), r, [out.key])

    def tt(self, out, in0, in1, op, eng="dve"):
        self.P.op(eng, lambda E: E.tensor_tensor(out=out.ap, in0=in0.ap, in1=in1.ap, op=op),
                  [in0.key, in1.key], [out.key])

    def ts(self, out, in0, s1, s2, op0, op1=None, eng="dve"):
        r = [in0.key]
        a1 = s1
        a2 = s2
        if isinstance(s1, V):
            r.append(s1.key)
            a1 = s1.ap
        if isinstance(s2, V):
            r.append(s2.key)
            a2 = s2.ap
        if op1 is None:
            self.P.op(eng, lambda E: E.tensor_scalar(out=out.ap, in0=in0.ap, scalar1=a1, scalar2=0.0, op0=op0,
                                                     op1=ALU.add), r, [out.key])
        else:
            self.P.op(eng, lambda E: E.tensor_scalar(out=out.ap, in0=in0.ap, scalar1=a1, scalar2=a2, op0=op0, op1=op1),
                      r, [out.key])

    def stt(self, out, in0, scalar, in1, op0, op1, eng="dve"):
        r = [in0.key, in1.key]
        a = scalar
        if isinstance(scalar, V):
            r.append(scalar.key)
            a = scalar.ap
        self.P.op(eng, lambda E: E.scalar_tensor_tensor(out=out.ap, in0=in0.ap, scalar=a, in1=in1.ap, op0=op0, op1=op1),
                  r, [out.key])

    def copy(self, out, in_, eng="dve"):
        if eng == "act":
            self.P.op("act", lambda E: E.copy(out=out.ap, in_=in_.ap), [in_.key], [out.key])
        else:
            self.P.op(eng, lambda E: E.tensor_copy(out=out.ap, in_=in_.ap), [in_.key], [out.key])

    def recip(self, out, in_):
        self.P.op("dve", lambda E: E.reciprocal(out=out.ap, in_=in_.ap), [in_.key], [out.key])

    def scan(self, out, d0, d1, initial, op0, op1):
        r = [d0.key, d1.key]
        a = initial
        if isinstance(initial, V):
            r.append(initial.key)
            a = initial.ap
        self.P.op("dve", lambda E: E.tensor_tensor_scan(out=out.ap, data0=d0.ap, data1=d1.ap, initial=a, op0=op0, op1=op1),
                  r, [out.key])

    def memset(self, out, val, eng="dve"):
        self.P.op(eng, lambda E: E.memset(out.ap, val), [], [out.key])

    def gather(self, out, src_ap, src_key, col):
        idx = self.idxt
        self.P.op("pool", lambda E: E.indirect_dma_start(
            out=out.ap, out_offset=None, in_=src_ap,
            in_offset=bass.IndirectOffsetOnAxis(ap=idx.ap[:, col:col + 1], axis=0)),
            [src_key, idx.key], [out.key], dma=True)

    def dma_dyn(self, out, in_fn, in_key):
        P = self.P
        self.P.op("sp", lambda E: E.dma_start(out=out.ap, in_=in_fn(P.jv["j"])), [in_key], [out.key], dma=True)

    def collective(self, out, in_, in_keys, groups):
        self.P.op("pool", lambda E: E.collective_compute("AllGather", ALU.bypass, replica_groups=groups,
                                                         ins=[in_.ap], outs=[out.ap]),
                  list(in_keys), [out.key], coll=True)

    def dma(self, out, in_, eng="sp", final=False, extra_reads=(), cast=False):
        self.P.op(eng, lambda E: E.dma_start(out=out.ap, in_=in_.ap), [in_.key] + list(extra_reads), [out.key],
                  dma=True, final=final, cast=cast)


class Pool:
    def __init__(self, tiles):
        self.t = tiles
        self.i = 0

    def get(self):
        v = self.t[self.i % len(self.t)]
        self.i += 1
        return v


class WStream:
    def __init__(self, cx, slots):
        self.cx = cx
        self.slots = slots
        self.items = []
        self.nxt = 0

    def extend(self, items):
        base = len(self.items)
        self.items.extend(items)
        return base

    def get(self, idx):
        n = len(self.slots)
        while self.nxt < len(self.items) and self.nxt <= idx + n - 1:
            src, w = self.items[self.nxt]
            slot = self.slots[self.nxt % n]
            self.cx.dma(slot[:, 0:w], src)
            self.nxt += 1
        return self.slots[idx % n]


def cast_weights(cx, wsrc, wdst, pieces):
    for (a, b) in pieces:
        cx.dma(wdst[:, a:b].k(("wb", a)), wsrc[:, a:b], eng="pool")


def wview(wdst, pieces, off, w):
    for (a, b) in pieces:
        if a <= off and off + w <= b:
            return wdst[:, off:off + w].k(("wb", a))
    raise AssertionError((off, w))


def rmsnorm(cx, env, src, wcols, dst, ntok, inv_n=1.0 / 1024, nk=8):
    pss = env["psum"].get()
    for kc in range(nk):
        sq = env["bft"].get()
        cx.act(sq[:, 0:ntok], src(kc), AF.Square)
        cx.mm(pss[:, 0:ntok], env["ones"], sq[:, 0:ntok], start=(kc == 0), stop=(kc == nk - 1))
    t = env["f32t"].get()
    cx.act(t[:, 0:ntok], pss[:, 0:ntok], AF.Sqrt, bias=EPS, scale=inv_n)
    rstd = env["f32t"].get()
    cx.recip(rstd[:, 0:ntok], t[:, 0:ntok])
    for kc in range(nk):
        cx.stt(dst(kc), src(kc), wcols(kc), rstd[:, 0:ntok], ALU.mult, ALU.mult)
    return rstd


def ffn(cx, env, xt, hT, actT, ws, wl, base_s, base_l):
    for f in range(22):
        w = ws.get(base_s + f)
        wv = V(w.ap.rearrange("p (g k n) -> p g k n", g=2, k=8), w.key)
        pg = env["psum"].get()
        pu = env["psum"].get()
        for kc in range(8):
            cx.mm(pg, wv[:, 0, kc, :], hT[kc], start=(kc == 0), stop=(kc == 7))
        for kc in range(8):
            cx.mm(pu, wv[:, 1, kc, :], hT[kc], start=(kc == 0), stop=(kc == 7))
        sg = env["f32t"].get()
        cx.act(sg, pg, AF.Silu)
        cx.tt(actT[f], sg, pu, ALU.mult)
    for m in range(8):
        w = wl.get(base_l + m)
        wv = V(w.ap[:, 0:2816].rearrange("p (f n) -> p f n", f=22), w.key)
        po = env["psum"].get()
        for f in range(22):
            cx.mm(po, wv[:, f, :], actT[f], start=(f == 0), stop=(f == 21))
        cx.stt(xt(m), po, 0.5, xt(m), ALU.mult, ALU.add)


def common_env(cx, nps=7, nf=10, nb=8):
    env = {}
    env["psum"] = Pool([cx.ps("ps%d" % i, [128, 512], F32) for i in range(nps)])
    env["f32t"] = Pool(cx.sbs("f32t", [128, 512], F32, nf))
    env["bft"] = Pool(cx.sbs("bft", [128, 512], BF16, nb))
    return env


def load_const(cx, name, dram, shape, dt):
    t = cx.sb(name, shape, dt)
    cx.dma(t, dram)
    return t


_GB = [0, 6, 12, 17, 22]
A_WGU = 0
A_WD = A_WGU + 22 * 2048
A_WINF = A_WD + 8 * 2816
A_WINT = A_WINF + 20 * 1024
A_WFF = A_WINT + 2 * 4096
WA_COLS = A_WFF + 64
B0 = WA_COLS
B_WOUT = B0
B_WQ = B_WOUT + 8192
B_WK = B_WQ + 8192
B_WO = B_WK + 8192
B_WV = B_WO + 8192
B_WGU = B_WV + 8192
B_WD = B_WGU + 22 * 2048
W_COLS = B_WD + 8 * 2816
def _pieces(base, nchunk, width, per):
    return [(base + c * width, base + min(c + per, nchunk) * width) for c in range(0, nchunk, per)]


_wf = _pieces(A_WINF, 20, 1024, 4)
PIECES_A = (_pieces(A_WGU, 22, 2048, 2) + _pieces(A_WD, 8, 2816, 1) + _wf[3:] + [(A_WINT + 4096, WA_COLS)] +
            _wf[:3] + [(A_WINT, A_WINT + 4096)])
PIECES_B = (_pieces(B_WK, 8, 1024, 4) + _pieces(B_WV, 2, 4096, 1) + _pieces(B_WOUT, 8, 1024, 4) +
            _pieces(B_WQ, 8, 1024, 4) + _pieces(B_WO, 8, 1024, 4) + _pieces(B_WGU, 22, 2048, 2) +
            _pieces(B_WD, 8, 2816, 1))
PIECES = PIECES_A + PIECES_B
GROUPS = [[0, 1, 2, 3], [4, 5, 6, 7]]
STOP = None
NOCOLL = False
RCH = {"HG": 256, "VH": 128, "QA": 140, "KA": 140, "VF": 256, "AB": 512, "TO": 8, "O2": 64}


def _grow(row, r, rc):
    return (row // rc) * 4 * rc + r * rc + row % rc
NVEC = 2 * 41 + 8
U32 = mybir.dt.uint32


def _idx_names():
    n = []
    n += [("hg", r, k, half) for r in range(4) for k in range(4) for half in range(2)]
    n += [("vh", r, half) for r in range(4) for half in range(2)]
    n += [("ab", r, w) for r in range(4) for w in range(2)]
    n += [("tot", r, e) for r in range(4) for e in range(2)]
    n += [("qa", r, e) for r in range(4) for e in range(2)]
    n += [("va", r, e) for r in range(4) for e in range(2)]
    n += [("o2", kc, i) for kc in range(8) for i in range(4)]
    return n


IDX = {nm: i for i, nm in enumerate(_idx_names())}
NIDX = len(IDX)


def _idx_table(j):
    p = np.arange(128)
    t = np.zeros((128, NIDX), np.uint32)
    for nm, col in IDX.items():
        if nm[0] == "hg":
            _, r, k, half = nm
            v = 2 * _grow(j * 512 + k * 128 + p, r, RCH["HG"]) + half
        elif nm[0] == "vh":
            _, r, half = nm
            v = 2 * _grow(j * 64 + (p % 64), r, RCH["VH"]) + half
        elif nm[0] == "ab":
            _, r, w = nm
            v = 2 * (r * 512 + j * 128 + p) + w
        elif nm[0] == "tot":
            _, r, e = nm
            v = np.full(128, r * 8 + 2 * j + e)
        elif nm[0] == "qa":
            _, r, e = nm
            v = _grow((2 * j + e) * 70 + np.minimum(p, 69), r, RCH["QA"])
        elif nm[0] == "va":
            _, r, e = nm
            v = _grow((2 * j + e) * 128 + p, r, RCH["VF"])
        else:
            _, kc, i = nm
            v = 16 * _grow((kc // 4) * 128 + p, kc % 4, RCH["O2"]) + j * 4 + i
        t[:, col] = v
    return t


class Arena:
    def __init__(self, cx, name, n, dt):
        self.t = cx.sb(name, [128, n], dt)
        self.n = n
        self.off = 0
        self.gen = 0

    def reset(self):
        self.off = 0
        self.gen += 1

    def get(self, name, shape):
        size = 1
        for d in shape[1:]:
            size *= d
        ap = self.t.ap[0:shape[0], self.off:self.off + size]
        self.off += (size + 15) // 16 * 16
        assert self.off <= self.n, (name, self.off, self.n)
        if len(shape) == 3:
            ap = ap.rearrange("p (a b) -> p a b", a=shape[1])
        elif len(shape) == 4:
            ap = ap.rearrange("p (a b c) -> p a b c", a=shape[1], b=shape[2])
        return V(ap, (name, self.gen))

    def gets(self, name, shape, n):
        return [self.get("%s%d" % (name, i), shape) for i in range(n)]


def build_fused():
    cx = Ctx()
    nc = cx.nc
    xin = cx.dram("xT", [1024, 2048], F32, "ExternalInput")
    min_ = cx.dram("memT", [1024, 256], F32, "ExternalInput")
    wsrc = [cx.dram("w%d" % l, [128, W_COLS], F32, "ExternalInput") for l in range(2)]
    vecs = cx.dram("vecs", [128, NVEC], F32, "ExternalInput")
    lbp = cx.dram("lbp", [128, 8], F32, "ExternalInput")
    fb = cx.dram("fb", [8, 2], F32, "ExternalInput")
    cones = cx.dram("cones", [128, 128], BF16, "ExternalInput")
    cident = cx.dram("cident", [128, 128], BF16, "ExternalInput")
    cmask = cx.dram("cmask", [128, 128], BF16, "ExternalInput")
    ctri = cx.dram("ctri", [64, 64], BF16, "ExternalInput")
    cscan = cx.dram("cscan", [128, 512], F32, "ExternalInput")
    cones3 = cx.dram("cones3", [3, 8192], BF16, "ExternalInput")
    idxd = cx.dram("idxd", [128, NIDX], U32, "ExternalInput")
    yo = cx.dram("yo", [1024, 2048], F32, "ExternalOutput")
    wb = [cx.dram("wb%d" % l, [128, W_COLS], BF16) for l in range(2)]
    HGs = cx.dram("HGs", [2048, 2048], BF16)
    HGg = cx.dram("HGg", [4 * 2048, 2048], BF16)
    VHs = cx.dram("VHs", [256, 4096], BF16)
    VHg = cx.dram("VHg", [4 * 256, 4096], BF16)
    QAs = cx.dram("QAs", [560, 2048], BF16)
    QAg = cx.dram("QAg", [4 * 560, 2048], BF16)
    KAs = cx.dram("KAs", [560, 2048], BF16)
    KAg = cx.dram("KAg", [4 * 560, 2048], BF16)
    VFs = cx.dram("VFs", [1024, 2048], BF16)
    VFg = cx.dram("VFg", [4 * 1024, 2048], BF16)
    ABs = cx.dram("ABs", [512, 64], F32)
    ABg = cx.dram("ABg", [4 * 512, 64], F32)
    TOs = cx.dram("TOs", [8, 16], F32)
    TOg = cx.dram("TOg", [4 * 8, 16], F32)
    O2s = cx.dram("O2s", [256, 8192], BF16)
    O2g = cx.dram("O2g", [4 * 256, 8192], BF16)

    with cx.es:
        P = cx.P
        P.persist = {"HG", "VH", "QA", "KA", "VF", "AB", "O2", "QA1", "KA1", "wb", "yo", "HGg", "VHg", "QAg", "KAg",
                     "VFg", "ABg", "O2g", "TO", "TOs", "TOg", "HGs", "VHs", "QAs", "KAs", "VFs", "ABs", "O2s", "w0", "w1"}
        env = {}
        psl = [cx.ps("ps%d" % i, [128, 512], F32) for i in range(7)]
        pTR = cx.ps("pTR", [128, 1024], BF16)
        env["psum"] = Pool(psl)
        castq = [(l, a, b) for l in range(2) for (a, b) in PIECES]
        castpos = [0]

        def cast_next(n):
            for _ in range(n):
                if castpos[0] < len(castq):
                    l_, a, b = castq[castpos[0]]
                    castpos[0] += 1
                    xr = [("x", kc) for kc in range(8)] if first_cast[0] else []
                    first_cast[0] = False
                    cx.dma(wb[l_][:, a:b].k(("wb", l_, a)), wsrc[l_][:, a:b], eng="pool", cast=True, extra_reads=xr)

        first_cast = [True]

        def wv_(l, off, w):
            for (a, b) in PIECES:
                if a <= off and off + w <= b:
                    return (wb[l][:, off:off + w].k(("wb", l, a)), w)
            raise AssertionError((off, w))

        xT = cx.sb("xTs", [128, 8, 2048], F32)
        xin_v = V(xin.ap.rearrange("(k p) t -> p k t", p=128), xin.key)
        for kc in range(8):
            cx.dma(xT[:, kc, :].k(("x", kc)), xin_v[:, kc, :])
        cast_next(len(PIECES_A))
        env["ones"] = load_const(cx, "ones", cones, [128, 128], BF16)
        ident = load_const(cx, "ident", cident, [128, 128], BF16)
        maskn = load_const(cx, "maskn", cmask, [128, 128], BF16)
        tri = load_const(cx, "tri", ctri, [64, 64], BF16)
        scanm = load_const(cx, "scanm", cscan, [128, 512], F32)
        vec = load_const(cx, "vec", vecs, [128, NVEC], F32)
        lb_in = load_const(cx, "lb_in", lbp, [128, 8], F32)
        fbt = load_const(cx, "fbt", fb, [8, 2], F32)
        lbd = cx.sb("lbd", [128, 4], F32)
        cx.tt(lbd, lb_in[:, 4:8], lb_in[:, 0:4], ALU.subtract)
        lbv = cx.sb("lbv", [128, 2, 4], F32)
        cx.memset(lbv, 0.0)
        cx.act(lbv[:, 1, :], lbd, AF.Sigmoid)
        oml = cx.sb("oml", [128, 2, 4], F32)
        cx.ts(oml, lbv, -1.0, 1.0, ALU.mult, ALU.add)
        nfb = cx.sb("nfb", [8, 2], F32)
        cx.ts(nfb, fbt, -1.0, None, ALU.mult)
        onesf = cx.sb("onesf", [1, 128], F32)
        cx.memset(onesf, 1.0)
        allone = cx.sb("allone", [8, 512], F32)
        cx.memset(allone, 1.0)
        vtl = [cx.sb("vtl%d" % i, [128, 8, 128], BF16) for i in range(2)]
        for t in vtl:
            cx.memset(t, 1.0)
        idxt = load_const(cx, "idxt", idxd, [128, NIDX], U32)
        cx.idxt = idxt
        QAv = QAs.ap.rearrange("(h s) t -> h s t", s=70)
        KAv = KAs.ap.rearrange("(h s) t -> h s t", s=70)
        for hh in range(8):
            cx.dma(V(QAv[hh, 67:70, :], ("QA1", hh)), cones3[:, 0:2048])
            cx.dma(V(KAv[hh, 64:67, :], ("KA1", hh)), cones3[:, 0:2048])
        abf = Arena(cx, "abf", 42240, BF16)
        af3 = Arena(cx, "af3", 6400, F32)
        xo_keys = []
        deferred = []

        def chunked_gather(nm, g_, s_, keys_, only=None):
            rc = RCH[nm]
            rows = s_.ap.shape[0]
            for c in (range(rows // rc) if only is None else only):
                cx.collective(V(g_.ap[c * 4 * rc:(c + 1) * 4 * rc, :], g_.key), V(s_.ap[c * rc:(c + 1) * rc, :], s_.key),
                              keys_, GROUPS)

        def vcol(l, which, kc):
            c = l * 41 + which * 8 + kc
            return vec[:, c:c + 1]

        def phase_A(l):
            abf.reset()
            af3.reset()
            env["f32t"] = Pool(af3.gets("f32t", [128, 512], 10))
            env["bft"] = Pool(abf.gets("bft", [128, 512], 8))
            hTt = abf.get("hT", [128, 8, 512])
            hT = [hTt[:, kc, :].k(("hT", l, kc)) for kc in range(8)]
            actTt = abf.get("actT", [128, 22, 512])
            actT = [actTt[:, f, :].k(("actT", l, f)) for f in range(22)]
            ws = WStream(cx, abf.gets("ws", [128, 2048], 3))
            wl = WStream(cx, abf.gets("wl", [128, 4096], 2))
            cbt = Pool(abf.gets("cbt", [8, 512], 8))
            wff = abf.get("wff", [128, 64])
            abt = af3.get("abt", [128, 4, 64])
            ngt = Pool(af3.gets("ngt", [128, 8], 2))
            ccar = af3.get("ccar", [8, 4])
            tot = af3.get("tot", [8, 16])
            wffv = V(wff.ap.rearrange("p (k n) -> p k n", k=8), wff.key)
            keys = {"HG": [], "VH": [], "QA": [("QA1", hh) for hh in range(8)],
                    "KA": [("KA1", hh) for hh in range(8)], "VF": [], "AB": []}
            VHv = VHs.ap.rearrange("(h s) (c v) -> s c h v", s=64, v=128)
            VFv = VFs.ap.rearrange("(h p) (b d) -> p b h d", p=128, d=128)
            for i in range(4):
                tsl = slice(i * 512, (i + 1) * 512)
                xt = lambda m: xT[:, m, tsl].k(("x", m))
                bs = ws.extend([wv_(l, A_WGU + f * 2048, 2048) for f in range(22)] +
                               [wv_(l, A_WINF + c * 1024, 1024) for c in range(12, 20)])
                bl = wl.extend([wv_(l, A_WD + m * 2816, 2816) for m in range(8)] +
                               [wv_(l, A_WINT + 4096, 4096)])
                rmsnorm(cx, env, xt, lambda kc: vcol(l, 0, kc), lambda kc: hT[kc], 512)
                ffn(cx, env, xt, hT, actT, ws, wl, bs, bl)
                rmsnorm(cx, env, xt, lambda kc: vcol(l, 1, kc), lambda kc: hT[kc], 512)
                pc = [0]

                def proj(M=128):
                    w = ws.get(bs + 22 + pc[0])
                    pc[0] += 1
                    wv = V(w.ap[:, 0:1024].rearrange("p (k n) -> p k n", k=8), w.key)
                    p = env["psum"].get()
                    for kc in range(8):
                        cx.mm(p[0:M, :], wv[:, kc, 0:M], hT[kc], start=(kc == 0), stop=(kc == 7))
                    return p

                for qk in range(2):
                    for c in range(4):
                        p = proj()
                        t = env["bft"].get()
                        cx.copy(t, p, eng="act")
                        dstv = QAv if qk == 0 else KAv
                        nm = "QA" if qk == 0 else "KA"
                        for e_ in range(2):
                            k_ = (nm, "qk", c, e_, i)
                            keys[nm].append(k_)
                            cx.dma(V(dstv[2 * c + e_, 0:64, tsl], k_), t[e_ * 64:(e_ + 1) * 64, :])
                if i == 0:
                    cx.dma(wff, wv_(l, A_WFF, 64)[0])
                pff = env["psum"].get()
                for kc in range(8):
                    cx.mm(pff[0:8, :], wffv[:, kc, :], hT[kc], start=(kc == 0), stop=(kc == 7))
                g8 = lambda: env["f32t"].get()[0:8, :]
                ee = g8()
                cx.act(ee, pff[0:8, :], AF.Exp, bias=nfb[:, l:l + 1], scale=-1.0)
                cx.ts(ee, ee, 1.0, None, ALU.add)
                sp = g8()
                cx.act(sp, ee, AF.Ln)
                cp = g8()
                if i == 0:
                    cx.scan(cp, allone, sp, 0.0, ALU.mult, ALU.add)
                else:
                    cx.scan(cp, allone, sp, ccar[:, i - 1:i], ALU.mult, ALU.add)
                cx.copy(ccar[:, i:i + 1], cp[:, 511:512])
                c8 = g8()
                cx.ts(c8, cp, -8.0, None, ALU.mult)
                parts = []
                cur = c8
                for part in range(3):
                    hb = cbt.get()
                    cx.copy(hb, cur)
                    parts.append(hb)
                    if part < 2:
                        nxt = g8()
                        cx.tt(nxt, cur, hb, ALU.subtract)
                        cur = nxt
                for part in range(3):
                    nb = cbt.get()
                    cx.ts(nb, parts[part], -1.0, None, ALU.mult)
                    parts.append(nb)
                for part in range(6):
                    nm = "QA" if part < 3 else "KA"
                    k_ = (nm, "c", part, i)
                    keys[nm].append(k_)
                    if part < 3:
                        cx.dma(V(QAv[:, 64 + part, tsl], k_), parts[part])
                    else:
                        cx.dma(V(KAv[:, 67 + part - 3, tsl], k_), parts[part])
                for which in (1,):
                    w = wl.get(bl + 8)
                    wv = V(w.ap.rearrange("p (k n) -> p k n", k=8), w.key)
                    for blk in range(4):
                        p = env["psum"].get()
                        for kc in range(8):
                            cx.mm(p, hT[kc][:, blk * 128:(blk + 1) * 128], wv[:, kc, :], start=(kc == 0), stop=(kc == 7))
                        gb = i * 4 + blk
                        if which == 0:
                            t = env["bft"].get()
                            cx.copy(t, p, eng=("act" if blk % 2 == 0 else "dve"))
                            tv = V(t.ap.rearrange("p (h v) -> p h v", h=4), t.key)
                            for half in range(2):
                                k_ = ("VH", gb, half)
                                keys["VH"].append(k_)
                                cx.dma(V(VHv[:, gb * 2 + half, :, :], k_), tv[half * 64:(half + 1) * 64, :, :])
                        else:
                            t = vtl[blk % 2]
                            pv = V(p.ap.rearrange("p (h d) -> p h d", h=8), p.key)
                            cx.copy(t[:, :, 0:64], pv, eng=("act" if blk % 2 == 0 else "dve"))
                            k_ = ("VF", gb)
                            keys["VF"].append(k_)
                            cx.dma(V(VFv[:, gb, :, :], k_), t)
            cx.memset(tot, 0.0)
            cx.ts(tot[:, 0:1], ccar[:, 3:4], -1.0, None, ALU.mult)
            keys["TO"] = [("TO", "tot")]
            cx.dma(V(TOs.ap[0:8, 0:16], ("TO", "tot")), tot)
            for nm, s_, g_ in (("TO", TOs, TOg), ("QA", QAs, QAg), ("KA", KAs, KAg), ("VF", VFs, VFg)):
                chunked_gather(nm, g_, s_, keys[nm])
            for i in range(4):
                tsl = slice(i * 512, (i + 1) * 512)
                xt = lambda m: xT[:, m, tsl].k(("x", m))
                bs = ws.extend([wv_(l, A_WINF + c * 1024, 1024) for c in range(12)]) - 22
                bl = wl.extend([wv_(l, A_WINT, 4096)]) - 8
                rmsnorm(cx, env, xt, lambda kc: vcol(l, 1, kc), lambda kc: hT[kc], 512)
                pass
                pc = [0]

                def proj(M=128):
                    w = ws.get(bs + 22 + pc[0])
                    pc[0] += 1
                    wv = V(w.ap[:, 0:1024].rearrange("p (k n) -> p k n", k=8), w.key)
                    p = env["psum"].get()
                    for kc in range(8):
                        cx.mm(p[0:M, :], wv[:, kc, 0:M], hT[kc], start=(kc == 0), stop=(kc == 7))
                    return p

                for h in range(4):
                    pq = proj()
                    pf = proj()
                    pgt = proj()
                    sg = env["f32t"].get()
                    cx.act(sg, pf, AF.Sigmoid)
                    f = env["f32t"].get()
                    cx.ts(f, sg, oml[:, l, h:h + 1], lbv[:, l, h:h + 1], ALU.mult, ALU.add)
                    lf = env["f32t"].get()
                    cx.act(lf, f, AF.Ln)
                    kk = env["f32t"].get()
                    cx.ts(kk, f, -1.0, 1.0, ALU.mult, ALU.add)
                    g = env["f32t"].get()
                    cx.scan(g, scanm, lf, 0.0, ALU.mult, ALU.add)
                    ng = ngt.get()
                    gv = V(g.ap.rearrange("p (c t) -> p c t", t=64), g.key)
                    cx.ts(ng, gv[:, :, 31], -1.0, None, ALU.mult)
                    dd = env["f32t"].get()
                    for c in range(8):
                        cx.ts(dd[:, c * 64:(c + 1) * 64], g[:, c * 64:(c + 1) * 64], ng[:, c:c + 1], None, ALU.add)
                    e1 = env["f32t"].get()
                    cx.act(e1, dd, AF.Exp)
                    e2 = env["f32t"].get()
                    cx.act(e2, dd, AF.Exp, scale=-1.0)
                    e3 = env["f32t"].get()
                    cx.act(e3, g, AF.Exp)
                    q = env["f32t"].get()
                    cx.act(q, pq, AF.Silu)
                    qg = env["bft"].get()
                    cx.tt(qg, q, e1, ALU.mult)
                    kg = env["bft"].get()
                    cx.tt(kg, kk, e2, ALU.mult)
                    qs = env["bft"].get()
                    cx.tt(qs, q, e3, ALU.mult)
                    gt = env["bft"].get()
                    cx.act(gt, pgt, AF.Silu)
                    e3v = V(e3.ap.rearrange("p (c t) -> p c t", t=64), e3.key)
                    e1v = V(e1.ap.rearrange("p (c t) -> p c t", t=64), e1.key)
                    cx.copy(abt[:, h, i * 8:(i + 1) * 8], e3v[:, :, 63])
                    cx.copy(abt[:, h, 32 + i * 8:32 + (i + 1) * 8], e1v[:, :, 63])
                    for kind, t in enumerate((qg, kg, qs, gt)):
                        r0 = h * 512 + kind * 128
                        k_ = ("HG", h, kind, i)
                        keys["HG"].append(k_)
                        cx.dma(V(HGs.ap[r0:r0 + 128, tsl], k_), t)
                for which in (0,):
                    w = wl.get(bl + 8)
                    wv = V(w.ap.rearrange("p (k n) -> p k n", k=8), w.key)
                    for blk in range(4):
                        p = env["psum"].get()
                        for kc in range(8):
                            cx.mm(p, hT[kc][:, blk * 128:(blk + 1) * 128], wv[:, kc, :], start=(kc == 0), stop=(kc == 7))
                        gb = i * 4 + blk
                        if which == 0:
                            t = env["bft"].get()
                            cx.copy(t, p, eng=("act" if blk % 2 == 0 else "dve"))
                            tv = V(t.ap.rearrange("p (h v) -> p h v", h=4), t.key)
                            for half in range(2):
                                k_ = ("VH", gb, half)
                                keys["VH"].append(k_)
                                cx.dma(V(VHv[:, gb * 2 + half, :, :], k_), tv[half * 64:(half + 1) * 64, :, :])
                        else:
                            t = vtl[blk % 2]
                            pv = V(p.ap.rearrange("p (h d) -> p h d", h=8), p.key)
                            cx.copy(t[:, :, 0:64], pv, eng=("act" if blk % 2 == 0 else "dve"))
                            k_ = ("VF", gb)
                            keys["VF"].append(k_)
                            cx.dma(V(VFv[:, gb, :, :], k_), t)
            for h in range(4):
                k_ = ("AB", h)
                keys["AB"].append(k_)
                cx.dma(V(ABs.ap[h * 128:(h + 1) * 128, :], k_), abt[:, h, :])
            deferred.append(lambda: [chunked_gather("AB", ABg, ABs, keys["AB"]),
                                     chunked_gather("HG", HGg, HGs, keys["HG"], only=range(0, 4))])
            deferred.append(lambda: [chunked_gather("HG", HGg, HGs, keys["HG"], only=range(4, 8)),
                                     chunked_gather("VH", VHg, VHs, keys["VH"])])

        def phase_M(l):
            P.barrier()
            abf.reset()
            af3.reset()
            pS = [psl[0], psl[1], psl[3]]
            pO = [psl[2]]
            pAT, pHO, pSU = psl[4], psl[5], psl[6]
            env["f32t"] = Pool(af3.gets("f32t", [128, 512], 6))
            env["bft"] = Pool(abf.gets("bft", [128, 512], 4))
            ab = af3.get("ab", [128, 2, 128])
            tg = af3.get("tg", [128, 8, 16])
            dcol = af3.get("dcol", [128, 2, 4, 4])
            S = af3.get("S", [128, 128])
            St = af3.get("St", [128, 128])
            ohat = af3.get("ohat", [128, 1024])
            rcp = Pool(af3.gets("rc", [128, 512], 2))
            Sb = abf.get("Sb", [128, 128])
            hslots = [[abf.get("hg%d_%d" % (k, s), [128, 1024]) for k in range(4)] for s in range(2)]
            vslots = [abf.get("vc%d" % s, [128, 2048]) for s in range(2)]
            QA = abf.get("QA", [128, 8192])
            KA = abf.get("KA", [128, 8192])
            VA = abf.get("VA", [128, 64, 128])
            ptp = Pool(abf.gets("pt", [128, 512], 3))
            ofp = Pool(abf.gets("ofs", [64, 512], 2))
            onw = vec[:, l * 41 + 40:l * 41 + 41]
            ABg2 = ABg.ap.rearrange("r (h c) -> (r h) c", h=2)
            for r in range(4):
                for e in range(2):
                    cx.gather(tg[:, r * 2 + e, :], TOg.ap, TOg.key, IDX[("tot", r, e)])
            cx.memset(dcol, 0.0)
            for e in range(2):
                for rq in range(4):
                    for rk in range(rq - 1, -1, -1):
                        cx.tt(dcol[:, e, rq, rk:rk + 1], dcol[:, e, rq, rk + 1:rk + 2],
                              tg[:, rk * 2 + e, 0:1], ALU.add)
            cx.memset(S, 0.0)
            cx.memset(Sb, 0.0)
            NSEG = 8
            o2keys = []
            o2keys_f = [[], []]

            def hgrn_load(u):
                s = u % 2
                r, half = u // 2, u % 2
                csl = slice(half * 1024, (half + 1) * 1024)
                HGg2 = HGg.ap.rearrange("r (h c) -> (r h) c", h=2)
                VHg2 = VHg.ap.rearrange("r (h c) -> (r h) c", h=2)
                for k in range(4):
                    cx.gather(hslots[s][k], HGg2, HGg.key, IDX[("hg", r, k, half)])
                cx.gather(vslots[s], VHg2, VHg.key, IDX[("vh", r, half)])

            AT2 = [abf.get("ATs%d" % i_, [64, 64]) for i_ in range(2)]
            KT2 = [abf.get("KgT%d" % i_, [64, 128]) for i_ in range(2)]
            for t_ in AT2:
                cx.memset(t_, 0.0)

            def hgrn_stage1(u, c):
                s = u % 2
                Qg, Kg, Qs, _ = hslots[s]
                t0 = c * 64
                b_ = c % 2
                pa = pAT[:, 0:64]
                ptr = pTR[:, 0:128]
                cx.mm(pa[0:64, 32:64], Kg[:, t0:t0 + 64], Qg[:, t0 + 32:t0 + 64])
                cx.mm(pa[0:32, 0:32], Kg[:, t0:t0 + 32], Qg[:, t0:t0 + 32])
                cx.tt(AT2[b_][0:64, 32:64], pa[0:64, 32:64], tri[0:64, 32:64], ALU.mult)
                cx.tt(AT2[b_][0:32, 0:32], pa[0:32, 0:32], tri[0:32, 0:32], ALU.mult)
                cx.transpose(ptr[0:64, :], Kg[:, t0:t0 + 64], ident)
                cx.copy(KT2[b_], ptr[0:64, :], eng="dve")

            def hgrn_stage2(u, c):
                s = u % 2
                Qg, Kg, Qs, _ = hslots[s]
                Vc = V(vslots[s].ap[0:64, :].rearrange("p (c v) -> p c v", v=128), vslots[s].key)
                t0 = c * 64
                cc = u * 16 + c
                b_ = c % 2
                cx.mm(pHO[:, 0:64], Sb, Qs[:, t0:t0 + 64], start=True, stop=False)
                cx.mm(pHO[:, 0:64], Vc[:, c, :], AT2[b_], start=False, stop=True)
                cx.copy(ohat[:, t0:t0 + 64], pHO[:, 0:64], eng="dve")
                cx.mm(pSU[:, 0:128], KT2[b_], Vc[:, c, :])
                cx.ts(St, S, ab[:, 0, cc:cc + 1], None, ALU.mult)
                cx.stt(S, pSU[:, 0:128], ab[:, 1, cc:cc + 1], St, ALU.mult, ALU.add)
                cx.copy(Sb, S, eng="dve")

            def hgrn_finish(u):
                s = u % 2
                gate = hslots[s][3]
                for pc_ in range(2):
                    sl = slice(pc_ * 512, (pc_ + 1) * 512)
                    sq = env["bft"].get()
                    cx.tt(sq, ohat[:, sl], ohat[:, sl], ALU.mult)
                    cx.mm(pSU, env["ones"], sq)
                    t = env["f32t"].get()
                    cx.act(t, pSU, AF.Ln, bias=EPS, scale=1.0 / 128)
                    rstd = env["f32t"].get()
                    cx.act(rstd, t, AF.Exp, scale=-0.5)
                    on = env["f32t"].get()
                    cx.stt(on, ohat[:, sl], onw, rstd, ALU.mult, ALU.mult)
                    og = env["bft"].get()
                    cx.tt(og, on, gate[:, sl], ALU.mult)
                    c0 = u * 1024 + pc_ * 512
                    k_ = ("O2", "h", u, pc_)
                    o2keys.append(k_)
                    cx.dma(V(O2s.ap[0:128, c0:c0 + 512], k_), og)

            def fox_load(e):
                for r in range(4):
                    sl = slice(r * 2048, (r + 1) * 2048)
                    cx.gather(QA[:, sl].k(("QA", l, r)), QAg.ap, QAg.key, IDX[("qa", r, e)])
                    cx.gather(KA[:, sl].k(("KA", l, r)), KAg.ap, KAg.key, IDX[("qa", r, e)])
                    vdst = V(VA.ap[:, r * 16:(r + 1) * 16, :].rearrange("p b d -> p (b d)"), ("VA", l, r))
                    cx.gather(vdst, VFg.ap, VFg.key, IDX[("va", r, e)])

            def fox_tile(e, g):
                rq = g // 4
                po = pO[0]
                nkb = 4 * g + 4

                def geom(kb):
                    diag = kb >= 4 * g
                    lo = (kb - 4 * g) * 128 if diag else 0
                    return diag, lo, 512 - lo, g * 512 + lo

                def s_stage(kb):
                    rk = kb // 16
                    diag, lo, w, q0 = geom(kb)
                    ps = pS[kb % 3]
                    rdk = [("KA", l, rk), ("QA", l, rq)]
                    Kap = KA[0:70, kb * 128:(kb + 1) * 128]
                    Qap = QA[0:70, q0:q0 + w]
                    P.op("pe", lambda E, ps=ps, Kap=Kap, Qap=Qap, w=w, diag=diag:
                         E.matmul(ps.ap[:, 0:w], Kap.ap, Qap.ap, start=True, stop=not diag), rdk, [ps.key])
                    if diag:
                        cx.mm(ps[:, 0:128], ident, maskn, start=False, stop=True)

                s_stage(0)
                s_stage(1)
                for kb in range(nkb):
                    if kb + 2 < nkb:
                        s_stage(kb + 2)
                    rk = kb // 16
                    diag, lo, w, q0 = geom(kb)
                    ps = pS[kb % 3]
                    pt = ptp.get()
                    cx.act(pt[:, 0:w], ps[:, 0:w], AF.Exp, bias=dcol[:, e, rq, rk:rk + 1], scale=0.125)
                    cx.mm(po[:, lo:512], VA[:, kb, :].k(("VA", l, rk)), pt[:, 0:w], start=(kb == 0), stop=(kb == nkb - 1))
                    unit_hook()
                rc = rcp.get()
                cx.recip(rc[64:128, :], po[64:128, :])
                of = ofp.get()
                cx.tt(of, po[0:64, :], rc[64:128, :], ALU.mult)
                k_ = ("O2", "f", e, g)
                o2keys_f[e].append(k_)
                cx.dma(V(O2s.ap[128 + e * 64:128 + (e + 1) * 64, g * 512:(g + 1) * 512], k_), of)

            fox_load(0)
            deferred.pop(0)()
            early_h = (l == 0)
            if early_h:
                deferred.pop(0)()
                for r in range(4):
                    for w_ in range(2):
                        cx.gather(ab[:, w_, r * 32:(r + 1) * 32], ABg2, ABg.key, IDX[("ab", r, w_)])
                hgrn_load(0)
                hgrn_load(1)
            hsteps = [(u, c) for u in range(NSEG) for c in range(16)]
            hstate = {"n": 0, "units": 0, "s1": False}

            def hgrn_step():
                n = hstate["n"]
                if n >= len(hsteps):
                    return
                if not hstate["s1"]:
                    hgrn_stage1(*hsteps[0])
                    hstate["s1"] = True
                if n + 1 < len(hsteps):
                    un, cn = hsteps[n + 1]
                    hgrn_stage1(un, cn)
                u, c = hsteps[n]
                hgrn_stage2(u, c)
                if c == 15:
                    hgrn_finish(u)
                    if u + 2 < NSEG:
                        hgrn_load(u + 2)
                hstate["n"] = n + 1

            def unit_hook():
                hstate["units"] += 1
                k = hstate["units"]
                k0, kstep = (576, 4) if l == 0 else (900, 2)
                if k >= k0 and (k - k0) % kstep == 0:
                    hgrn_step()

            for (e, g) in [(e, g) for e in range(2) for g in range(16)]:
                if e == 1 and g == 0:
                    fox_load(1)
                    if not early_h:
                        deferred.pop(0)()
                        for r in range(4):
                            for w_ in range(2):
                                cx.gather(ab[:, w_, r * 32:(r + 1) * 32], ABg2, ABg.key, IDX[("ab", r, w_)])
                        hgrn_load(0)
                        hgrn_load(1)
                fox_tile(e, g)
                if g == 15:
                    chunked_gather("O2", O2g, O2s, o2keys_f[e], only=[2 + e])
                if (e == 0 and 3 <= g <= 11) or (e == 1 and g >= 2):
                    cast_next(3)
            while hstate["n"] < len(hsteps):
                hgrn_step()
            chunked_gather("O2", O2g, O2s, o2keys, only=[0, 1])

        def phase_B(l, last):
            P.barrier()
            abf.reset()
            af3.reset()
            env["f32t"] = Pool(af3.gets("f32t", [128, 512], 8))
            env["bft"] = Pool(abf.gets("bft", [128, 512], 4))
            mT = af3.get("mT", [128, 8, 256])
            hTt = abf.get("hT", [128, 8, 512])
            hT = [hTt[:, kc, :].k(("hTb", l, kc)) for kc in range(8)]
            actTt = abf.get("actT", [128, 24, 512])
            actT = [actTt[:, f, :].k(("actTb", l, f)) for f in range(24)]
            ws = WStream(cx, abf.gets("ws", [128, 2048], 3))
            wl = WStream(cx, abf.gets("wl", [128, 4096], 2))
            mnT = abf.get("mnT", [128, 8, 256])
            KcT = abf.get("KcT", [128, 8, 256])
            Vc = abf.get("Vc", [128, 2, 1024])
            ptp = Pool(abf.gets("pt", [128, 512], 4))
            oTc = [actT[kc] for kc in range(8)]
            qTc = [actT[8 + kc] for kc in range(8)]
            ocTc = [actT[16 + kc] for kc in range(8)]
            cx.dma(mT, V(min_.ap.rearrange("(k p) t -> p k t", p=128), min_.key))

            def sq(off, c):
                return wv_(l, off + c * 1024, 1024)

            rmsnorm(cx, env, lambda kc: mT[:, kc, :], lambda kc: vcol(l, 3, kc),
                    lambda kc: mnT[:, kc, :].k(("mnT", l, kc)), 256)
            bs = ws.extend([sq(B_WK, m) for m in range(8)])
            bl = wl.extend([wv_(l, B_WV + c * 4096, 4096) for c in range(2)])
            for m in range(8):
                w = ws.get(bs + m)
                wv = V(w.ap[:, 0:1024].rearrange("p (k n) -> p k n", k=8), w.key)
                p = env["psum"].get()
                for kc in range(8):
                    cx.mm(p[:, 0:256], wv[:, kc, :], mnT[:, kc, :].k(("mnT", l, kc)), start=(kc == 0), stop=(kc == 7))
                cx.copy(KcT[:, m, :].k(("KcT", l, m)), p[:, 0:256], eng="act")
            for half in range(2):
                w = wl.get(bl + half)
                wv = V(w.ap.rearrange("p (k n) -> p k n", k=8), w.key)
                for mb in range(2):
                    p = env["psum"].get()
                    for kc in range(8):
                        cx.mm(p, mnT[:, kc, mb * 128:(mb + 1) * 128].k(("mnT", l, kc)), wv[:, kc, :],
                              start=(kc == 0), stop=(kc == 7))
                    cx.copy(Vc[:, mb, half * 512:(half + 1) * 512].k(("Vc", l, mb, half)), p, eng="dve")
            for i in range(4):
                tsl = slice(i * 512, (i + 1) * 512)
                xt = lambda m: xT[:, m, tsl].k(("x", m))
                bs = ws.extend([sq(B_WOUT, m) for m in range(8)] + [sq(B_WQ, m) for m in range(8)] +
                               [sq(B_WO, m) for m in range(8)] +
                               [wv_(l, B_WGU + f * 2048, 2048) for f in range(22)])
                bl = wl.extend([wv_(l, B_WD + m * 2816, 2816) for m in range(8)])
                O2g2 = O2g.ap.rearrange("r (h c) -> (r h) c", h=16)
                for kc in range(8):
                    cx.gather(oTc[kc], O2g2, O2g.key, IDX[("o2", kc, i)])

                def sqmm(base, rhs_fn, evac):
                    for m in range(8):
                        w = ws.get(base + m)
                        wv = V(w.ap[:, 0:1024].rearrange("p (k n) -> p k n", k=8), w.key)
                        p = env["psum"].get()
                        for kc in range(8):
                            cx.mm(p, wv[:, kc, :], rhs_fn(kc), start=(kc == 0), stop=(kc == 7))
                        evac(m, p)

                cast_next(4)
                sqmm(bs, lambda kc: oTc[kc], lambda m, p: cx.tt(xt(m), p, xt(m), ALU.add))
                rmsnorm(cx, env, xt, lambda kc: vcol(l, 2, kc), lambda kc: hT[kc], 512)
                sqmm(bs + 8, lambda kc: hT[kc],
                     lambda m, p: cx.copy(qTc[m], p, eng=("act" if m % 2 else "dve")))
                for hd in range(4):
                    pts = []
                    for mb in range(2):
                        p = env["psum"].get()
                        for dc in range(2):
                            m = hd * 2 + dc
                            cx.mm(p, KcT[:, m, mb * 128:(mb + 1) * 128].k(("KcT", l, m)), qTc[m],
                                  start=(dc == 0), stop=(dc == 1))
                        pt = ptp.get()
                        cx.act(pt, p, AF.Exp, scale=1.0 / 16)
                        pts.append(pt)
                    pl = env["psum"].get()
                    for mb in range(2):
                        cx.mm(pl, env["ones"], pts[mb], start=(mb == 0), stop=(mb == 1))
                    rc = env["f32t"].get()
                    cx.recip(rc, pl)
                    for dc in range(2):
                        m = hd * 2 + dc
                        p = env["psum"].get()
                        for mb in range(2):
                            cx.mm(p, Vc[:, mb, m * 128:(m + 1) * 128].k(("Vc", l, mb, m // 4)), pts[mb],
                                  start=(mb == 0), stop=(mb == 1))
                        cx.tt(ocTc[m], p, rc, ALU.mult)
                sqmm(bs + 16, lambda kc: ocTc[kc], lambda m, p: cx.tt(xt(m), p, xt(m), ALU.add))
                rmsnorm(cx, env, xt, lambda kc: vcol(l, 4, kc), lambda kc: hT[kc], 512)
                ffn(cx, env, xt, hT, actT, ws, wl, bs + 24, bl)
                if last:
                    rstd = rmsnorm(cx, env, xt, lambda kc: vec[:, 82 + kc:83 + kc], lambda kc: hT[kc], 512)
                    for m in range(8):
                        y = env["f32t"].get()
                        cx.stt(y, xt(m), vec[:, 82 + m:83 + m], rstd, ALU.mult, ALU.mult)
                        cx.dma(V(yo.ap.rearrange("(k p) t -> p k t", p=128)[:, m, tsl], ("yo", m, i)), y, final=True)

        def dump_x():
            P.barrier()
            for i in range(4):
                tsl = slice(i * 512, (i + 1) * 512)
                for m in range(8):
                    cx.dma(V(yo.ap.rearrange("(k p) t -> p k t", p=128)[:, m, tsl], ("yo", m, i)),
                           xT[:, m, tsl].k(("x", m)), final=True)

        done = False
        for l in range(2):
            phase_A(l)
            if STOP == "A%d" % l:
                dump_x()
                break
            phase_M(l)
            if STOP == "M%d" % l:
                dump_x()
                break
            phase_B(l, l == 1 and STOP is None)
            if STOP == "B%d" % l:
                dump_x()
                break
            if l == 0:
                P.barrier()
        cx.P.emit(nc, cx.es)
    return nc


_CACHE = {}


def _sqw(w):
    return np.ascontiguousarray(w.reshape(8, 128, 8, 128).transpose(1, 2, 0, 3)).reshape(128, 8192)


def _wgu(wg, wu):
    g = wg.reshape(8, 128, 22, 128).transpose(1, 2, 0, 3)
    u = wu.reshape(8, 128, 22, 128).transpose(1, 2, 0, 3)
    return np.ascontiguousarray(np.stack([g, u], axis=2)).reshape(128, 22 * 2048)


def _wd(wd):
    return np.ascontiguousarray(wd.reshape(22, 128, 8, 128).transpose(1, 2, 0, 3)).reshape(128, 8 * 2816)


def _vec(v):
    return np.ascontiguousarray(v.reshape(8, 128).T)


def kernel(x, mem, ffn1_norm, ffn1_w_gate, ffn1_w_up, ffn1_w_down, mix_norm, w_in, hgrn_lb, hgrn_out_norm,
           fox_f_bias, w_out, cross_norm, mem_norm, cross_wq, cross_wk, cross_wv, cross_wo, ffn2_norm,
           ffn2_w_gate, ffn2_w_up, ffn2_w_down, final_norm):
    f32 = np.float32
    A = lambda a: np.asarray(a, dtype=f32)
    x = A(x)
    mem = A(mem)
    ones = np.ones((128, 128), NPBF)
    ident = np.eye(128, dtype=f32).astype(NPBF)
    jj = np.arange(128)
    maskn = np.where(jj[:, None] > jj[None, :], -30000.0, 0.0).astype(NPBF)
    ss = np.arange(64)
    tri = (ss[:, None] <= ss[None, :]).astype(f32).astype(NPBF)
    scan = np.ones((128, 512), f32)
    scan[:, 0::64] = 0.0
    ones3 = np.ones((3, 8192), NPBF)
    wpk = []
    vcols = []
    for l in range(2):
        win = A(w_in[l])
        cols = []
        for h in range(4):
            for base in (0, 512, 1536):
                cols.append(np.arange(base + h * 128, base + (h + 1) * 128))
        cols.append(np.arange(2048, 3072))
        winf = win[:, np.concatenate(cols)]
        winf = np.ascontiguousarray(winf.reshape(8, 128, 20, 128).transpose(1, 2, 0, 3)).reshape(128, 20 * 1024)
        wint = win[:, np.r_[1024:1536, 3072:3584]]
        wint = np.ascontiguousarray(wint.reshape(8, 128, 2, 512).transpose(1, 2, 0, 3)).reshape(128, 2 * 4096)
        wff = np.ascontiguousarray(win[:, 3584:3592].reshape(8, 128, 8).transpose(1, 0, 2)).reshape(128, 64)
        wv = np.ascontiguousarray(A(cross_wv[l]).reshape(8, 128, 2, 512).transpose(1, 2, 0, 3)).reshape(128, 8192)
        w = np.concatenate([_wgu(A(ffn1_w_gate[l]), A(ffn1_w_up[l])), _wd(A(ffn1_w_down[l])), winf, wint, wff,
                            _sqw(A(w_out[l])), _sqw(A(cross_wq[l])), _sqw(A(cross_wk[l])), _sqw(A(cross_wo[l])), wv,
                            _wgu(A(ffn2_w_gate[l]), A(ffn2_w_up[l])), _wd(A(ffn2_w_down[l]))], axis=1)
        assert w.shape[1] == W_COLS
        wpk.append(np.ascontiguousarray(w))
        vcols += [_vec(A(ffn1_norm[l])), _vec(A(mix_norm[l])), _vec(A(cross_norm[l])), _vec(A(mem_norm[l])),
                  _vec(A(ffn2_norm[l])), A(hgrn_out_norm[l]).reshape(128, 1)]
    vcols.append(_vec(A(final_norm)))
    vecs = np.ascontiguousarray(np.concatenate(vcols, axis=1))
    assert vecs.shape[1] == NVEC
    lb = A(hgrn_lb)
    lbp = np.ascontiguousarray(np.concatenate([lb[0].reshape(4, 128).T, lb[1].reshape(4, 128).T], axis=1))
    fbv = np.ascontiguousarray(A(fox_f_bias).T)
    if "F" not in _CACHE:
        _CACHE["F"] = build_fused()
    in_maps = []
    for c in range(8):
        b, j = c // 4, c % 4
        in_maps.append({"xT": np.ascontiguousarray(x[b, j * 2048:(j + 1) * 2048, :].T),
                        "memT": np.ascontiguousarray(mem[b].T), "w0": wpk[0], "w1": wpk[1], "vecs": vecs,
                        "lbp": lbp, "fb": fbv, "cones": ones, "cident": ident, "cmask": maskn, "ctri": tri,
                        "cscan": scan, "cones3": ones3, "idxd": _idx_table(j)})
    res = run_bass_kernel_spmd(_CACHE["F"], in_maps, core_ids=list(range(8)))
    out = np.zeros((2, 8192, 1024), f32)
    for c in range(8):
        out[c // 4, (c % 4) * 2048:(c % 4 + 1) * 2048, :] = np.asarray(res.results[c]["yo"]).T
    return out
```

```python
import contextlib
import numpy as np
import ml_dtypes
import concourse.bass as bass
import concourse.mybir as mybir
from concourse.bass_utils import run_bass_kernel_spmd

F32 = mybir.dt.float32
BF16 = mybir.dt.bfloat16
ALU = mybir.AluOpType
AF = mybir.ActivationFunctionType
NPBF = ml_dtypes.bfloat16

NDMA = 32
EPS = 1e-6


class Op:
    __slots__ = ("eng", "fn", "deps", "sig", "sigval", "is_dma", "dsem", "dval", "extra")

    def __init__(self, eng, fn, is_dma=False):
        self.eng = eng
        self.fn = fn
        self.deps = set()
        self.sig = False
        self.sigval = 0
        self.is_dma = is_dma
        self.dsem = None
        self.dval = 0


class V:
    __slots__ = ("ap", "key")

    def __init__(self, ap, key):
        self.ap = ap
        self.key = key

    def __getitem__(self, idx):
        return V(self.ap[idx], self.key)

    def k(self, key):
        return V(self.ap, key)


class Prog:
    ENGS = ("pe", "act", "dve", "pool", "sp")

    def __init__(self):
        self.q = {e: [] for e in self.ENGS}
        self.lastw = {}
        self.readers = {}
        self.dma_rr = 0
        self.dma_rr2 = 0
        self.dma_rr3 = 0
        self.dma_last = [None] * NDMA
        self.dma_cnt = [0] * NDMA
        self.out_dmas = []
        self.ncoll = 0
        self.coll_last = []
        self.pending = {}
        self.jv = {}
        self.persist = set()

    def barrier(self):
        lasts = set()
        for e in self.ENGS:
            for o in reversed(self.q[e]):
                if not o.is_dma and o.fn is not None:
                    o.sig = True
                    lasts.add(o)
                    break
        for o in self.dma_last:
            if o is not None:
                lasts.add(o)
        self.pending = {e: set(lasts) for e in self.ENGS}
        pk = lambda k: (k if isinstance(k, str) else k[0]) in self.persist
        self.lastw = {k: v for k, v in self.lastw.items() if pk(k)}
        self.readers = {k: v for k, v in self.readers.items() if pk(k)}

    def op(self, eng, fn, reads=(), writes=(), dma=False, final=False, coll=False, cast=False):
        o = Op(eng, fn, dma or coll)
        o.extra = 16
        deps = set(self.pending.pop(eng, ()))
        for r in reads:
            w = self.lastw.get(r)
            if w is not None:
                deps.add(w)
        for wk in writes:
            w = self.lastw.get(wk)
            if w is not None:
                deps.add(w)
            rd = self.readers.get(wk)
            if rd:
                deps.update(rd[0].values())
                deps.update(rd[1])
        if dma:
            if eng == "sp":
                k = self.dma_rr
                self.dma_rr = (k + 1) % 12
            elif cast:
                k = 24 + self.dma_rr3
                self.dma_rr3 = (self.dma_rr3 + 1) % 4
            else:
                k = 12 + self.dma_rr2
                self.dma_rr2 = (self.dma_rr2 + 1) % 12
            if self.dma_last[k] is not None:
                deps.add(self.dma_last[k])
            self.dma_cnt[k] += 1
            o.dsem = k
            o.dval = 16 * self.dma_cnt[k]
            self.dma_last[k] = o
        if coll:
            if self.coll_last:
                deps.add(self.coll_last[-1])
            o.dsem = NDMA
            self.ncoll += 1
            o.dval = self.ncoll
            o.extra = 1
            self.coll_last.append(o)
        for r in reads:
            rd = self.readers.setdefault(r, ({}, []))
            if dma or coll:
                rd[1].append(o)
            else:
                rd[0][eng] = o
        for wk in writes:
            self.lastw[wk] = o
            self.readers[wk] = ({}, [])
        deps.discard(o)
        if eng == "pe" and not dma:
            deps = {d for d in deps if d.is_dma or d.eng != "pe"}
        for d in deps:
            if not d.is_dma:
                d.sig = True
        o.deps = deps
        self.q[eng].append(o)
        if final:
            self.out_dmas.append(o)
        return o

    def emit(self, nc, es):
        esem = {e: es.enter_context(nc.semaphore("s_" + e)) for e in self.ENGS}
        dsem = [es.enter_context(nc.semaphore("d%d" % i)) for i in range(NDMA + 1)]
        for e in self.ENGS:
            c = 0
            for o in self.q[e]:
                if o.sig and not o.is_dma:
                    c += 1
                    o.sigval = c
        fin = Op("sp", None)
        fin.deps = set(self.out_dmas)
        self.q["sp"].append(fin)
        block = es.enter_context(nc.Block())
        prog = self

        def run(e, E):
            waited = {}
            if e == "sp":
                prog.jv["j"] = E.partition_id() % 4
            for o in prog.q[e]:
                need = {}
                for d in o.deps:
                    if d.is_dma:
                        s, v = ("d", d.dsem), d.dval
                    else:
                        s, v = ("e", d.eng), d.sigval
                    if need.get(s, 0) < v:
                        need[s] = v
                for s, v in need.items():
                    if waited.get(s, 0) >= v:
                        continue
                    waited[s] = v
                    E.wait_ge(dsem[s[1]] if s[0] == "d" else esem[s[1]], v)
                if o.fn is None:
                    continue
                ins = o.fn(E)
                if o.is_dma:
                    ins.then_inc(dsem[o.dsem], o.extra)
                elif o.sig:
                    ins.then_inc(esem[e], 1)

        @block.tensor
        def _(E):
            run("pe", E)

        @block.scalar
        def _(E):
            run("act", E)

        @block.vector
        def _(E):
            run("dve", E)

        @block.gpsimd
        def _(E):
            run("pool", E)

        @block.sync
        def _(E):
            run("sp", E)


class Ctx:
    def __init__(self):
        self.nc = bass.Bass("TRN2", target_bir_lowering=False)
        self.P = Prog()
        self.es = contextlib.ExitStack()
        self.nkey = 0

    def sb(self, name, shape, dt, key=None):
        t = self.es.enter_context(self.nc.sbuf_tensor(name, list(shape), dt))
        return V(t[:], key or name)

    def sbs(self, name, shape, dt, n):
        return [self.sb("%s%d" % (name, i), shape, dt) for i in range(n)]

    def ps(self, name, shape, dt):
        t = self.es.enter_context(self.nc.psum_tensor(name, list(shape), dt))
        return V(t[:], name)

    def dram(self, name, shape, dt, kind="Internal"):
        t = self.nc.dram_tensor(name, list(shape), dt, kind=kind)
        return V(t.ap(), name)

    def mm(self, out, lhsT, rhs, start=True, stop=True):
        r = [lhsT.key, rhs.key] + ([] if start else [out.key])
        self.P.op("pe", lambda E: E.matmul(out.ap, lhsT.ap, rhs.ap, start=start, stop=stop), r, [out.key])

    def transpose(self, out, in_, ident):
        self.P.op("pe", lambda E: E.transpose(out.ap, in_.ap, ident.ap), [in_.key, ident.key], [out.key])

    def act(self, out, in_, func, bias=None, scale=None):
        r = [in_.key]
        kw = {}
        if bias is not None:
            if isinstance(bias, V):
                r.append(bias.key)
                kw["bias"] = bias.ap
            else:
                kw["bias"] = float(bias)
        if scale is not None:
            kw["scale"] = float(scale)
        self.P.op("act", lambda E: E.activation(out=out.ap, in_=in_.ap, func=func, **kw), r, [out.key])

    def tt(self, out, in0, in1, op, eng="dve"):
        self.P.op(eng, lambda E: E.tensor_tensor(out=out.ap, in0=in0.ap, in1=in1.ap, op=op),
                  [in0.key, in1.key], [out.key])

    def ts(self, out, in0, s1, s2, op0, op1=None, eng="dve"):
        r = [in0.key]
        a1 = s1
        a2 = s2
        if isinstance(s1, V):
            r.append(s1.key)
            a1 = s1.ap
        if isinstance(s2, V):
            r.append(s2.key)
            a2 = s2.ap
        if op1 is None:
            self.P.op(eng, lambda E: E.tensor_scalar(out=out.ap, in0=in0.ap, scalar1=a1, scalar2=0.0, op0=op0,
                                                     op1=ALU.add), r, [out.key])
        else:
            self.P.op(eng, lambda E: E.tensor_scalar(out=out.ap, in0=in0.ap, scalar1=a1, scalar2=a2, op0=op0, op1=op1),
                      r, [out.key])

    def stt(self, out, in0, scalar, in1, op0, op1, eng="dve"):
        r = [in0.key, in1.key]
        a = scalar
        if isinstance(scalar, V):
            r.append(scalar.key)
            a = scalar.ap
        self.P.op(eng, lambda E: E.scalar_tensor_tensor(out=out.ap, in0=in0.ap, scalar=a, in1=in1.ap, op0=op0, op1=op1),
                  r, [out.key])

    def copy(self, out, in_, eng="dve"):
        if eng == "act":
            self.P.op("act", lambda E: E.copy(out=out.ap, in_=in_.ap), [in_.key], [out.key])
        else:
            self.P.op(eng, lambda E: E.tensor_copy(out=out.ap, in_=in_.ap), [in_.key], [out.key])

    def recip(self, out, in_):
        self.P.op("dve", lambda E: E.reciprocal(out=out.ap, in_=in_.ap), [in_.key], [out.key])

    def scan(self, out, d0, d1, initial, op0, op1):
        r = [d0.key, d1.key]
        a = initial
        if isinstance(initial, V):
            r.append(initial.key)
            a = initial.ap
        self.P.op("dve", lambda E: E.tensor_tensor_scan(out=out.ap, data0=d0.ap, data1=d1.ap, initial=a, op0=op0, op1=op1),
                  r, [out.key])

    def memset(self, out, val, eng="dve"):
        self.P.op(eng, lambda E: E.memset(out.ap, val), [], [out.key])

    def gather(self, out, src_ap, src_key, col):
        idx = self.idxt
        self.P.op("pool", lambda E: E.indirect_dma_start(
            out=out.ap, out_offset=None, in_=src_ap,
            in_offset=bass.IndirectOffsetOnAxis(ap=idx.ap[:, col:col + 1], axis=0)),
            [src_key, idx.key], [out.key], dma=True)

    def dma_dyn(self, out, in_fn, in_key):
        P = self.P
        self.P.op("sp", lambda E: E.dma_start(out=out.ap, in_=in_fn(P.jv["j"])), [in_key], [out.key], dma=True)

    def collective(self, out, in_, in_keys, groups):
        self.P.op("pool", lambda E: E.collective_compute("AllGather", ALU.bypass, replica_groups=groups,
                                                         ins=[in_.ap], outs=[out.ap]),
                  list(in_keys), [out.key], coll=True)

    def dma(self, out, in_, eng="sp", final=False, extra_reads=(), cast=False):
        self.P.op(eng, lambda E: E.dma_start(out=out.ap, in_=in_.ap), [in_.key] + list(extra_reads), [out.key],
                  dma=True, final=final, cast=cast)


class Pool:
    def __init__(self, tiles):
        self.t = tiles
        self.i = 0

    def get(self):
        v = self.t[self.i % len(self.t)]
        self.i += 1
        return v


class WStream:
    def __init__(self, cx, slots):
        self.cx = cx
        self.slots = slots
        self.items = []
        self.nxt = 0

    def extend(self, items):
        base = len(self.items)
        self.items.extend(items)
        return base

    def get(self, idx):
        n = len(self.slots)
        while self.nxt < len(self.items) and self.nxt <= idx + n - 1:
            src, w = self.items[self.nxt]
            slot = self.slots[self.nxt % n]
            self.cx.dma(slot[:, 0:w], src)
            self.nxt += 1
        return self.slots[idx % n]


def cast_weights(cx, wsrc, wdst, pieces):
    for (a, b) in pieces:
        cx.dma(wdst[:, a:b].k(("wb", a)), wsrc[:, a:b], eng="pool")


def wview(wdst, pieces, off, w):
    for (a, b) in pieces:
        if a <= off and off + w <= b:
            return wdst[:, off:off + w].k(("wb", a))
    raise AssertionError((off, w))


def rmsnorm(cx, env, src, wcols, dst, ntok, inv_n=1.0 / 1024, nk=8):
    pss = env["psum"].get()
    for kc in range(nk):
        sq = env["bft"].get()
        cx.act(sq[:, 0:ntok], src(kc), AF.Square)
        cx.mm(pss[:, 0:ntok], env["ones"], sq[:, 0:ntok], start=(kc == 0), stop=(kc == nk - 1))
    t = env["f32t"].get()
    cx.act(t[:, 0:ntok], pss[:, 0:ntok], AF.Sqrt, bias=EPS, scale=inv_n)
    rstd = env["f32t"].get()
    cx.recip(rstd[:, 0:ntok], t[:, 0:ntok])
    for kc in range(nk):
        cx.stt(dst(kc), src(kc), wcols(kc), rstd[:, 0:ntok], ALU.mult, ALU.mult)
    return rstd


def ffn(cx, env, xt, hT, actT, ws, wl, base_s, base_l):
    for f in range(22):
        w = ws.get(base_s + f)
        wv = V(w.ap.rearrange("p (g k n) -> p g k n", g=2, k=8), w.key)
        pg = env["psum"].get()
        pu = env["psum"].get()
        for kc in range(8):
            cx.mm(pg, wv[:, 0, kc, :], hT[kc], start=(kc == 0), stop=(kc == 7))
        for kc in range(8):
            cx.mm(pu, wv[:, 1, kc, :], hT[kc], start=(kc == 0), stop=(kc == 7))
        sg = env["f32t"].get()
        cx.act(sg, pg, AF.Silu)
        cx.tt(actT[f], sg, pu, ALU.mult)
    for m in range(8):
        w = wl.get(base_l + m)
        wv = V(w.ap[:, 0:2816].rearrange("p (f n) -> p f n", f=22), w.key)
        po = env["psum"].get()
        for f in range(22):
            cx.mm(po, wv[:, f, :], actT[f], start=(f == 0), stop=(f == 21))
        cx.stt(xt(m), po, 0.5, xt(m), ALU.mult, ALU.add)


def common_env(cx, nps=7, nf=10, nb=8):
    env = {}
    env["psum"] = Pool([cx.ps("ps%d" % i, [128, 512], F32) for i in range(nps)])
    env["f32t"] = Pool(cx.sbs("f32t", [128, 512], F32, nf))
    env["bft"] = Pool(cx.sbs("bft", [128, 512], BF16, nb))
    return env


def load_const(cx, name, dram, shape, dt):
    t = cx.sb(name, shape, dt)
    cx.dma(t, dram)
    return t


_GB = [0, 6, 12, 17, 22]
A_WGU = 0
A_WD = A_WGU + 22 * 2048
A_WINF = A_WD + 8 * 2816
A_WINT = A_WINF + 20 * 1024
A_WFF = A_WINT + 2 * 4096
WA_COLS = A_WFF + 64
B0 = WA_COLS
B_WOUT = B0
B_WQ = B_WOUT + 8192
B_WK = B_WQ + 8192
B_WO = B_WK + 8192
B_WV = B_WO + 8192
B_WGU = B_WV + 8192
B_WD = B_WGU + 22 * 2048
W_COLS = B_WD + 8 * 2816
def _pieces(base, nchunk, width, per):
    return [(base + c * width, base + min(c + per, nchunk) * width) for c in range(0, nchunk, per)]


_wf = _pieces(A_WINF, 20, 1024, 4)
PIECES_A = (_pieces(A_WGU, 22, 2048, 2) + _pieces(A_WD, 8, 2816, 1) + _wf[3:] + [(A_WINT + 4096, WA_COLS)] +
            _wf[:3] + [(A_WINT, A_WINT + 4096)])
PIECES_B = (_pieces(B_WK, 8, 1024, 4) + _pieces(B_WV, 2, 4096, 1) + _pieces(B_WOUT, 8, 1024, 4) +
            _pieces(B_WQ, 8, 1024, 4) + _pieces(B_WO, 8, 1024, 4) + _pieces(B_WGU, 22, 2048, 2) +
            _pieces(B_WD, 8, 2816, 1))
PIECES = PIECES_A + PIECES_B
GROUPS = [[0, 1, 2, 3], [4, 5, 6, 7]]
STOP = None
NOCOLL = False
RCH = {"HG": 256, "VH": 128, "QA": 140, "KA": 140, "VF": 256, "AB": 512, "TO": 8, "O2": 64}


def _grow(row, r, rc):
    return (row // rc) * 4 * rc + r * rc + row % rc
NVEC = 2 * 41 + 8
U32 = mybir.dt.uint32


def _idx_names():
    n = []
    n += [("hg", r, k, half) for r in range(4) for k in range(4) for half in range(2)]
    n += [("vh", r, half) for r in range(4) for half in range(2)]
    n += [("ab", r, w) for r in range(4) for w in range(2)]
    n += [("tot", r, e) for r in range(4) for e in range(2)]
    n += [("qa", r, e) for r in range(4) for e in range(2)]
    n += [("va", r, e) for r in range(4) for e in range(2)]
    n += [("o2", kc, i) for kc in range(8) for i in range(4)]
    return n


IDX = {nm: i for i, nm in enumerate(_idx_names())}
NIDX = len(IDX)


def _idx_table(j):
    p = np.arange(128)
    t = np.zeros((128, NIDX), np.uint32)
    for nm, col in IDX.items():
        if nm[0] == "hg":
            _, r, k, half = nm
            v = 2 * _grow(j * 512 + k * 128 + p, r, RCH["HG"]) + half
        elif nm[0] == "vh":
            _, r, half = nm
            v = 2 * _grow(j * 64 + (p % 64), r, RCH["VH"]) + half
        elif nm[0] == "ab":
            _, r, w = nm
            v = 2 * (r * 512 + j * 128 + p) + w
        elif nm[0] == "tot":
            _, r, e = nm
            v = np.full(128, r * 8 + 2 * j + e)
        elif nm[0] == "qa":
            _, r, e = nm
            v = _grow((2 * j + e) * 70 + np.minimum(p, 69), r, RCH["QA"])
        elif nm[0] == "va":
            _, r, e = nm
            v = _grow((2 * j + e) * 128 + p, r, RCH["VF"])
        else:
            _, kc, i = nm
            v = 16 * _grow((kc // 4) * 128 + p, kc % 4, RCH["O2"]) + j * 4 + i
        t[:, col] = v
    return t


class Arena:
    def __init__(self, cx, name, n, dt):
        self.t = cx.sb(name, [128, n], dt)
        self.n = n
        self.off = 0
        self.gen = 0

    def reset(self):
        self.off = 0
        self.gen += 1

    def get(self, name, shape):
        size = 1
        for d in shape[1:]:
            size *= d
        ap = self.t.ap[0:shape[0], self.off:self.off + size]
        self.off += (size + 15) // 16 * 16
        assert self.off <= self.n, (name, self.off, self.n)
        if len(shape) == 3:
            ap = ap.rearrange("p (a b) -> p a b", a=shape[1])
        elif len(shape) == 4:
            ap = ap.rearrange("p (a b c) -> p a b c", a=shape[1], b=shape[2])
        return V(ap, (name, self.gen))

    def gets(self, name, shape, n):
        return [self.get("%s%d" % (name, i), shape) for i in range(n)]


def build_fused():
    cx = Ctx()
    nc = cx.nc
    xin = cx.dram("xT", [1024, 2048], F32, "ExternalInput")
    min_ = cx.dram("memT", [1024, 256], F32, "ExternalInput")
    wsrc = [cx.dram("w%d" % l, [128, W_COLS], F32, "ExternalInput") for l in range(2)]
    vecs = cx.dram("vecs", [128, NVEC], F32, "ExternalInput")
    lbp = cx.dram("lbp", [128, 8], F32, "ExternalInput")
    fb = cx.dram("fb", [8, 2], F32, "ExternalInput")
    cones = cx.dram("cones", [128, 128], BF16, "ExternalInput")
    cident = cx.dram("cident", [128, 128], BF16, "ExternalInput")
    cmask = cx.dram("cmask", [128, 128], BF16, "ExternalInput")
    ctri = cx.dram("ctri", [64, 64], BF16, "ExternalInput")
    cscan = cx.dram("cscan", [128, 512], F32, "ExternalInput")
    cones3 = cx.dram("cones3", [3, 8192], BF16, "ExternalInput")
    idxd = cx.dram("idxd", [128, NIDX], U32, "ExternalInput")
    yo = cx.dram("yo", [1024, 2048], F32, "ExternalOutput")
    wb = [cx.dram("wb%d" % l, [128, W_COLS], BF16) for l in range(2)]
    HGs = cx.dram("HGs", [2048, 2048], BF16)
    HGg = cx.dram("HGg", [4 * 2048, 2048], BF16)
    VHs = cx.dram("VHs", [256, 4096], BF16)
    VHg = cx.dram("VHg", [4 * 256, 4096], BF16)
    QAs = cx.dram("QAs", [560, 2048], BF16)
    QAg = cx.dram("QAg", [4 * 560, 2048], BF16)
    KAs = cx.dram("KAs", [560, 2048], BF16)
    KAg = cx.dram("KAg", [4 * 560, 2048], BF16)
    VFs = cx.dram("VFs", [1024, 2048], BF16)
    VFg = cx.dram("VFg", [4 * 1024, 2048], BF16)
    ABs = cx.dram("ABs", [512, 64], F32)
    ABg = cx.dram("ABg", [4 * 512, 64], F32)
    TOs = cx.dram("TOs", [8, 16], F32)
    TOg = cx.dram("TOg", [4 * 8, 16], F32)
    O2s = cx.dram("O2s", [256, 8192], BF16)
    O2g = cx.dram("O2g", [4 * 256, 8192], BF16)

    with cx.es:
        P = cx.P
        P.persist = {"HG", "VH", "QA", "KA", "VF", "AB", "O2", "QA1", "KA1", "wb", "yo", "HGg", "VHg", "QAg", "KAg",
                     "VFg", "ABg", "O2g", "TO", "TOs", "TOg", "HGs", "VHs", "QAs", "KAs", "VFs", "ABs", "O2s", "w0", "w1"}
        env = {}
        psl = [cx.ps("ps%d" % i, [128, 512], F32) for i in range(7)]
        pTR = cx.ps("pTR", [128, 1024], BF16)
        env["psum"] = Pool(psl)
        castq = [(l, a, b) for l in range(2) for (a, b) in PIECES]
        castpos = [0]

        def cast_next(n):
            for _ in range(n):
                if castpos[0] < len(castq):
                    l_, a, b = castq[castpos[0]]
                    castpos[0] += 1
                    xr = [("x", kc) for kc in range(8)] if first_cast[0] else []
                    first_cast[0] = False
                    cx.dma(wb[l_][:, a:b].k(("wb", l_, a)), wsrc[l_][:, a:b], eng="pool", cast=True, extra_reads=xr)

        first_cast = [True]

        def wv_(l, off, w):
            for (a, b) in PIECES:
                if a <= off and off + w <= b:
                    return (wb[l][:, off:off + w].k(("wb", l, a)), w)
            raise AssertionError((off, w))

        xT = cx.sb("xTs", [128, 8, 2048], F32)
        xin_v = V(xin.ap.rearrange("(k p) t -> p k t", p=128), xin.key)
        for kc in range(8):
            cx.dma(xT[:, kc, :].k(("x", kc)), xin_v[:, kc, :])
        cast_next(len(PIECES_A))
        env["ones"] = load_const(cx, "ones", cones, [128, 128], BF16)
        ident = load_const(cx, "ident", cident, [128, 128], BF16)
        maskn = load_const(cx, "maskn", cmask, [128, 128], BF16)
        tri = load_const(cx, "tri", ctri, [64, 64], BF16)
        scanm = load_const(cx, "scanm", cscan, [128, 512], F32)
        vec = load_const(cx, "vec", vecs, [128, NVEC], F32)
        lb_in = load_const(cx, "lb_in", lbp, [128, 8], F32)
        fbt = load_const(cx, "fbt", fb, [8, 2], F32)
        lbd = cx.sb("lbd", [128, 4], F32)
        cx.tt(lbd, lb_in[:, 4:8], lb_in[:, 0:4], ALU.subtract)
        lbv = cx.sb("lbv", [128, 2, 4], F32)
        cx.memset(lbv, 0.0)
        cx.act(lbv[:, 1, :], lbd, AF.Sigmoid)
        oml = cx.sb("oml", [128, 2, 4], F32)
        cx.ts(oml, lbv, -1.0, 1.0, ALU.mult, ALU.add)
        nfb = cx.sb("nfb", [8, 2], F32)
        cx.ts(nfb, fbt, -1.0, None, ALU.mult)
        onesf = cx.sb("onesf", [1, 128], F32)
        cx.memset(onesf, 1.0)
        allone = cx.sb("allone", [8, 512], F32)
        cx.memset(allone, 1.0)
        vtl = [cx.sb("vtl%d" % i, [128, 8, 128], BF16) for i in range(2)]
        for t in vtl:
            cx.memset(t, 1.0)
        idxt = load_const(cx, "idxt", idxd, [128, NIDX], U32)
        cx.idxt = idxt
        QAv = QAs.ap.rearrange("(h s) t -> h s t", s=70)
        KAv = KAs.ap.rearrange("(h s) t -> h s t", s=70)
        for hh in range(8):
            cx.dma(V(QAv[hh, 67:70, :], ("QA1", hh)), cones3[:, 0:2048])
            cx.dma(V(KAv[hh, 64:67, :], ("KA1", hh)), cones3[:, 0:2048])
        abf = Arena(cx, "abf", 45056, BF16)
        af3 = Arena(cx, "af3", 6400, F32)
        xo_keys = []
        deferred = []

        def chunked_gather(nm, g_, s_, keys_, only=None):
            rc = RCH[nm]
            rows = s_.ap.shape[0]
            for c in (range(rows // rc) if only is None else only):
                cx.collective(V(g_.ap[c * 4 * rc:(c + 1) * 4 * rc, :], g_.key), V(s_.ap[c * rc:(c + 1) * rc, :], s_.key),
                              keys_, GROUPS)

        def vcol(l, which, kc):
            c = l * 41 + which * 8 + kc
            return vec[:, c:c + 1]

        def phase_A(l):
            abf.reset()
            af3.reset()
            env["f32t"] = Pool(af3.gets("f32t", [128, 512], 10))
            env["bft"] = Pool(abf.gets("bft", [128, 512], 8))
            hTt = abf.get("hT", [128, 8, 512])
            hT = [hTt[:, kc, :].k(("hT", l, kc)) for kc in range(8)]
            actTt = abf.get("actT", [128, 22, 512])
            actT = [actTt[:, f, :].k(("actT", l, f)) for f in range(22)]
            ws = WStream(cx, abf.gets("ws", [128, 2048], 3))
            wl = WStream(cx, abf.gets("wl", [128, 4096], 3))
            cbt = Pool(abf.gets("cbt", [8, 512], 8))
            wff = abf.get("wff", [128, 64])
            abt = af3.get("abt", [128, 4, 64])
            ngt = Pool(af3.gets("ngt", [128, 8], 2))
            ccar = af3.get("ccar", [8, 4])
            tot = af3.get("tot", [8, 16])
            wffv = V(wff.ap.rearrange("p (k n) -> p k n", k=8), wff.key)
            keys = {"HG": [], "VH": [], "QA": [("QA1", hh) for hh in range(8)],
                    "KA": [("KA1", hh) for hh in range(8)], "VF": [], "AB": []}
            VHv = VHs.ap.rearrange("(h s) (c v) -> s c h v", s=64, v=128)
            VFv = VFs.ap.rearrange("(h p) (b d) -> p b h d", p=128, d=128)
            for i in range(4):
                tsl = slice(i * 512, (i + 1) * 512)
                xt = lambda m: xT[:, m, tsl].k(("x", m))
                bs = ws.extend([wv_(l, A_WGU + f * 2048, 2048) for f in range(22)] +
                               [wv_(l, A_WINF + c * 1024, 1024) for c in range(12, 20)])
                bl = wl.extend([wv_(l, A_WD + m * 2816, 2816) for m in range(8)] +
                               [wv_(l, A_WINT + 4096, 4096)])
                rmsnorm(cx, env, xt, lambda kc: vcol(l, 0, kc), lambda kc: hT[kc], 512)
                ffn(cx, env, xt, hT, actT, ws, wl, bs, bl)
                rmsnorm(cx, env, xt, lambda kc: vcol(l, 1, kc), lambda kc: hT[kc], 512)
                pc = [0]

                def proj(M=128):
                    w = ws.get(bs + 22 + pc[0])
                    pc[0] += 1
                    wv = V(w.ap[:, 0:1024].rearrange("p (k n) -> p k n", k=8), w.key)
                    p = env["psum"].get()
                    for kc in range(8):
                        cx.mm(p[0:M, :], wv[:, kc, 0:M], hT[kc], start=(kc == 0), stop=(kc == 7))
                    return p

                for qk in range(2):
                    for c in range(4):
                        p = proj()
                        t = env["bft"].get()
                        cx.copy(t, p, eng="act")
                        dstv = QAv if qk == 0 else KAv
                        nm = "QA" if qk == 0 else "KA"
                        for e_ in range(2):
                            k_ = (nm, "qk", c, e_, i)
                            keys[nm].append(k_)
                            cx.dma(V(dstv[2 * c + e_, 0:64, tsl], k_), t[e_ * 64:(e_ + 1) * 64, :])
                if i == 0:
                    cx.dma(wff, wv_(l, A_WFF, 64)[0])
                pff = env["psum"].get()
                for kc in range(8):
                    cx.mm(pff[0:8, :], wffv[:, kc, :], hT[kc], start=(kc == 0), stop=(kc == 7))
                g8 = lambda: env["f32t"].get()[0:8, :]
                ee = g8()
                cx.act(ee, pff[0:8, :], AF.Exp, bias=nfb[:, l:l + 1], scale=-1.0)
                cx.ts(ee, ee, 1.0, None, ALU.add)
                sp = g8()
                cx.act(sp, ee, AF.Ln)
                cp = g8()
                if i == 0:
                    cx.scan(cp, allone, sp, 0.0, ALU.mult, ALU.add)
                else:
                    cx.scan(cp, allone, sp, ccar[:, i - 1:i], ALU.mult, ALU.add)
                cx.copy(ccar[:, i:i + 1], cp[:, 511:512])
                c8 = g8()
                cx.ts(c8, cp, -8.0, None, ALU.mult)
                parts = []
                cur = c8
                for part in range(3):
                    hb = cbt.get()
                    cx.copy(hb, cur)
                    parts.append(hb)
                    if part < 2:
                        nxt = g8()
                        cx.tt(nxt, cur, hb, ALU.subtract)
                        cur = nxt
                for part in range(3):
                    nb = cbt.get()
                    cx.ts(nb, parts[part], -1.0, None, ALU.mult)
                    parts.append(nb)
                for part in range(6):
                    nm = "QA" if part < 3 else "KA"
                    k_ = (nm, "c", part, i)
                    keys[nm].append(k_)
                    if part < 3:
                        cx.dma(V(QAv[:, 64 + part, tsl], k_), parts[part])
                    else:
                        cx.dma(V(KAv[:, 67 + part - 3, tsl], k_), parts[part])
                for which in (1,):
                    w = wl.get(bl + 8)
                    wv = V(w.ap.rearrange("p (k n) -> p k n", k=8), w.key)
                    for blk in range(4):
                        p = env["psum"].get()
                        for kc in range(8):
                            cx.mm(p, hT[kc][:, blk * 128:(blk + 1) * 128], wv[:, kc, :], start=(kc == 0), stop=(kc == 7))
                        gb = i * 4 + blk
                        if which == 0:
                            t = env["bft"].get()
                            cx.copy(t, p, eng=("act" if blk % 2 == 0 else "dve"))
                            tv = V(t.ap.rearrange("p (h v) -> p h v", h=4), t.key)
                            for half in range(2):
                                k_ = ("VH", gb, half)
                                keys["VH"].append(k_)
                                cx.dma(V(VHv[:, gb * 2 + half, :, :], k_), tv[half * 64:(half + 1) * 64, :, :])
                        else:
                            t = vtl[blk % 2]
                            pv = V(p.ap.rearrange("p (h d) -> p h d", h=8), p.key)
                            cx.copy(t[:, :, 0:64], pv, eng=("act" if blk % 2 == 0 else "dve"))
                            k_ = ("VF", gb)
                            keys["VF"].append(k_)
                            cx.dma(V(VFv[:, gb, :, :], k_), t)
            cx.memset(tot, 0.0)
            cx.ts(tot[:, 0:1], ccar[:, 3:4], -1.0, None, ALU.mult)
            keys["TO"] = [("TO", "tot")]
            cx.dma(V(TOs.ap[0:8, 0:16], ("TO", "tot")), tot)
            for nm, s_, g_ in (("TO", TOs, TOg), ("QA", QAs, QAg), ("KA", KAs, KAg), ("VF", VFs, VFg)):
                chunked_gather(nm, g_, s_, keys[nm])
            for i in range(4):
                tsl = slice(i * 512, (i + 1) * 512)
                xt = lambda m: xT[:, m, tsl].k(("x", m))
                bs = ws.extend([wv_(l, A_WINF + c * 1024, 1024) for c in range(12)]) - 22
                bl = wl.extend([wv_(l, A_WINT, 4096)]) - 8
                rmsnorm(cx, env, xt, lambda kc: vcol(l, 1, kc), lambda kc: hT[kc], 512)
                pass
                pc = [0]

                def proj(M=128):
                    w = ws.get(bs + 22 + pc[0])
                    pc[0] += 1
                    wv = V(w.ap[:, 0:1024].rearrange("p (k n) -> p k n", k=8), w.key)
                    p = env["psum"].get()
                    for kc in range(8):
                        cx.mm(p[0:M, :], wv[:, kc, 0:M], hT[kc], start=(kc == 0), stop=(kc == 7))
                    return p

                for h in range(4):
                    pq = proj()
                    pf = proj()
                    pgt = proj()
                    sg = env["f32t"].get()
                    cx.act(sg, pf, AF.Sigmoid)
                    f = env["f32t"].get()
                    cx.ts(f, sg, oml[:, l, h:h + 1], lbv[:, l, h:h + 1], ALU.mult, ALU.add)
                    lf = env["f32t"].get()
                    cx.act(lf, f, AF.Ln)
                    kk = env["f32t"].get()
                    cx.ts(kk, f, -1.0, 1.0, ALU.mult, ALU.add)
                    g = env["f32t"].get()
                    cx.scan(g, scanm, lf, 0.0, ALU.mult, ALU.add)
                    ng = ngt.get()
                    gv = V(g.ap.rearrange("p (c t) -> p c t", t=64), g.key)
                    cx.ts(ng, gv[:, :, 31], -1.0, None, ALU.mult)
                    dd = env["f32t"].get()
                    for c in range(8):
                        cx.ts(dd[:, c * 64:(c + 1) * 64], g[:, c * 64:(c + 1) * 64], ng[:, c:c + 1], None, ALU.add)
                    e1 = env["f32t"].get()
                    cx.act(e1, dd, AF.Exp)
                    e2 = env["f32t"].get()
                    cx.act(e2, dd, AF.Exp, scale=-1.0)
                    e3 = env["f32t"].get()
                    cx.act(e3, g, AF.Exp)
                    q = env["f32t"].get()
                    cx.act(q, pq, AF.Silu)
                    qg = env["bft"].get()
                    cx.tt(qg, q, e1, ALU.mult)
                    kg = env["bft"].get()
                    cx.tt(kg, kk, e2, ALU.mult)
                    qs = env["bft"].get()
                    cx.tt(qs, q, e3, ALU.mult)
                    gt = env["bft"].get()
                    cx.act(gt, pgt, AF.Silu)
                    e3v = V(e3.ap.rearrange("p (c t) -> p c t", t=64), e3.key)
                    e1v = V(e1.ap.rearrange("p (c t) -> p c t", t=64), e1.key)
                    cx.copy(abt[:, h, i * 8:(i + 1) * 8], e3v[:, :, 63])
                    cx.copy(abt[:, h, 32 + i * 8:32 + (i + 1) * 8], e1v[:, :, 63])
                    for kind, t in enumerate((qg, kg, qs, gt)):
                        r0 = h * 512 + kind * 128
                        k_ = ("HG", h, kind, i)
                        keys["HG"].append(k_)
                        cx.dma(V(HGs.ap[r0:r0 + 128, tsl], k_), t)
                for which in (0,):
                    w = wl.get(bl + 8)
                    wv = V(w.ap.rearrange("p (k n) -> p k n", k=8), w.key)
                    for blk in range(4):
                        p = env["psum"].get()
                        for kc in range(8):
                            cx.mm(p, hT[kc][:, blk * 128:(blk + 1) * 128], wv[:, kc, :], start=(kc == 0), stop=(kc == 7))
                        gb = i * 4 + blk
                        if which == 0:
                            t = env["bft"].get()
                            cx.copy(t, p, eng=("act" if blk % 2 == 0 else "dve"))
                            tv = V(t.ap.rearrange("p (h v) -> p h v", h=4), t.key)
                            for half in range(2):
                                k_ = ("VH", gb, half)
                                keys["VH"].append(k_)
                                cx.dma(V(VHv[:, gb * 2 + half, :, :], k_), tv[half * 64:(half + 1) * 64, :, :])
                        else:
                            t = vtl[blk % 2]
                            pv = V(p.ap.rearrange("p (h d) -> p h d", h=8), p.key)
                            cx.copy(t[:, :, 0:64], pv, eng=("act" if blk % 2 == 0 else "dve"))
                            k_ = ("VF", gb)
                            keys["VF"].append(k_)
                            cx.dma(V(VFv[:, gb, :, :], k_), t)
            for h in range(4):
                k_ = ("AB", h)
                keys["AB"].append(k_)
                cx.dma(V(ABs.ap[h * 128:(h + 1) * 128, :], k_), abt[:, h, :])
            deferred.append(lambda: [chunked_gather("AB", ABg, ABs, keys["AB"]),
                                     chunked_gather("HG", HGg, HGs, keys["HG"], only=range(0, 4))])
            deferred.append(lambda: [chunked_gather("HG", HGg, HGs, keys["HG"], only=range(4, 8)),
                                     chunked_gather("VH", VHg, VHs, keys["VH"])])

        def phase_M(l):
            P.barrier()
            abf.reset()
            af3.reset()
            pS = [psl[0], psl[1], psl[3]]
            pO = [psl[2]]
            pAT, pHO, pSU = psl[4], psl[5], psl[6]
            env["f32t"] = Pool(af3.gets("f32t", [128, 512], 6))
            env["bft"] = Pool(abf.gets("bft", [128, 512], 4))
            ab = af3.get("ab", [128, 2, 128])
            tg = af3.get("tg", [128, 8, 16])
            dcol = af3.get("dcol", [128, 2, 4, 4])
            S = af3.get("S", [128, 128])
            St = af3.get("St", [128, 128])
            ohat = af3.get("ohat", [128, 1024])
            rcp = Pool(af3.gets("rc", [128, 512], 2))
            Sb = abf.get("Sb", [128, 128])
            hslots = [[abf.get("hg%d_%d" % (k, s), [128, 1024]) for k in range(4)] for s in range(2)]
            vslots = [abf.get("vc%d" % s, [128, 2048]) for s in range(2)]
            QA = abf.get("QA", [128, 8192])
            KA = abf.get("KA", [128, 8192])
            VA = abf.get("VA", [128, 64, 128])
            ptp = Pool(abf.gets("pt", [128, 512], 3))
            ofp = Pool(abf.gets("ofs", [64, 512], 2))
            onw = vec[:, l * 41 + 40:l * 41 + 41]
            ABg2 = ABg.ap.rearrange("r (h c) -> (r h) c", h=2)
            for r in range(4):
                for e in range(2):
                    cx.gather(tg[:, r * 2 + e, :], TOg.ap, TOg.key, IDX[("tot", r, e)])
            cx.memset(dcol, 0.0)
            for e in range(2):
                for rq in range(4):
                    for rk in range(rq - 1, -1, -1):
                        cx.tt(dcol[:, e, rq, rk:rk + 1], dcol[:, e, rq, rk + 1:rk + 2],
                              tg[:, rk * 2 + e, 0:1], ALU.add)
            cx.memset(S, 0.0)
            cx.memset(Sb, 0.0)
            NSEG = 8
            o2keys = []
            o2keys_f = [[], []]

            def hgrn_load(u):
                s = u % 2
                r, half = u // 2, u % 2
                csl = slice(half * 1024, (half + 1) * 1024)
                HGg2 = HGg.ap.rearrange("r (h c) -> (r h) c", h=2)
                VHg2 = VHg.ap.rearrange("r (h c) -> (r h) c", h=2)
                for k in range(4):
                    cx.gather(hslots[s][k], HGg2, HGg.key, IDX[("hg", r, k, half)])
                cx.gather(vslots[s], VHg2, VHg.key, IDX[("vh", r, half)])

            AT2 = [abf.get("ATs%d" % i_, [64, 64]) for i_ in range(2)]
            KT2 = [abf.get("KgT%d" % i_, [64, 128]) for i_ in range(2)]
            for t_ in AT2:
                cx.memset(t_, 0.0)

            def hgrn_stage1(u, c):
                s = u % 2
                Qg, Kg, Qs, _ = hslots[s]
                t0 = c * 64
                b_ = c % 2
                pa = pAT[:, 0:64]
                ptr = pTR[:, 0:128]
                cx.mm(pa[0:64, 32:64], Kg[:, t0:t0 + 64], Qg[:, t0 + 32:t0 + 64])
                cx.mm(pa[0:32, 0:32], Kg[:, t0:t0 + 32], Qg[:, t0:t0 + 32])
                cx.tt(AT2[b_][0:64, 32:64], pa[0:64, 32:64], tri[0:64, 32:64], ALU.mult)
                cx.tt(AT2[b_][0:32, 0:32], pa[0:32, 0:32], tri[0:32, 0:32], ALU.mult)
                cx.transpose(ptr[0:64, :], Kg[:, t0:t0 + 64], ident)
                cx.copy(KT2[b_], ptr[0:64, :], eng="dve")

            def hgrn_stage2(u, c):
                s = u % 2
                Qg, Kg, Qs, _ = hslots[s]
                Vc = V(vslots[s].ap[0:64, :].rearrange("p (c v) -> p c v", v=128), vslots[s].key)
                t0 = c * 64
                cc = u * 16 + c
                b_ = c % 2
                cx.mm(pHO[:, 0:64], Sb, Qs[:, t0:t0 + 64], start=True, stop=False)
                cx.mm(pHO[:, 0:64], Vc[:, c, :], AT2[b_], start=False, stop=True)
                cx.copy(ohat[:, t0:t0 + 64], pHO[:, 0:64], eng="dve")
                cx.mm(pSU[:, 0:128], KT2[b_], Vc[:, c, :])
                cx.ts(St, S, ab[:, 0, cc:cc + 1], None, ALU.mult)
                cx.stt(S, pSU[:, 0:128], ab[:, 1, cc:cc + 1], St, ALU.mult, ALU.add)
                cx.copy(Sb, S, eng="dve")

            def hgrn_finish(u):
                s = u % 2
                gate = hslots[s][3]
                for pc_ in range(2):
                    sl = slice(pc_ * 512, (pc_ + 1) * 512)
                    sq = env["bft"].get()
                    cx.tt(sq, ohat[:, sl], ohat[:, sl], ALU.mult)
                    cx.mm(pSU, env["ones"], sq)
                    t = env["f32t"].get()
                    cx.act(t, pSU, AF.Ln, bias=EPS, scale=1.0 / 128)
                    rstd = env["f32t"].get()
                    cx.act(rstd, t, AF.Exp, scale=-0.5)
                    on = env["f32t"].get()
                    cx.stt(on, ohat[:, sl], onw, rstd, ALU.mult, ALU.mult)
                    og = env["bft"].get()
                    cx.tt(og, on, gate[:, sl], ALU.mult)
                    c0 = u * 1024 + pc_ * 512
                    k_ = ("O2", "h", u, pc_)
                    o2keys.append(k_)
                    cx.dma(V(O2s.ap[0:128, c0:c0 + 512], k_), og)

            def fox_load(e):
                for r in range(4):
                    sl = slice(r * 2048, (r + 1) * 2048)
                    cx.gather(QA[:, sl].k(("QA", l, r)), QAg.ap, QAg.key, IDX[("qa", r, e)])
                    cx.gather(KA[:, sl].k(("KA", l, r)), KAg.ap, KAg.key, IDX[("qa", r, e)])
                    vdst = V(VA.ap[:, r * 16:(r + 1) * 16, :].rearrange("p b d -> p (b d)"), ("VA", l, r))
                    cx.gather(vdst, VFg.ap, VFg.key, IDX[("va", r, e)])

            def fox_tile(e, g):
                rq = g // 4
                po = pO[0]
                nkb = 4 * g + 4

                def geom(kb):
                    diag = kb >= 4 * g
                    lo = (kb - 4 * g) * 128 if diag else 0
                    return diag, lo, 512 - lo, g * 512 + lo

                def s_stage(kb):
                    rk = kb // 16
                    diag, lo, w, q0 = geom(kb)
                    ps = pS[kb % 3]
                    rdk = [("KA", l, rk), ("QA", l, rq)]
                    Kap = KA[0:70, kb * 128:(kb + 1) * 128]
                    Qap = QA[0:70, q0:q0 + w]
                    P.op("pe", lambda E, ps=ps, Kap=Kap, Qap=Qap, w=w, diag=diag:
                         E.matmul(ps.ap[:, 0:w], Kap.ap, Qap.ap, start=True, stop=not diag), rdk, [ps.key])
                    if diag:
                        cx.mm(ps[:, 0:128], ident, maskn, start=False, stop=True)

                s_stage(0)
                s_stage(1)
                for kb in range(nkb):
                    if kb + 2 < nkb:
                        s_stage(kb + 2)
                    rk = kb // 16
                    diag, lo, w, q0 = geom(kb)
                    ps = pS[kb % 3]
                    pt = ptp.get()
                    cx.act(pt[:, 0:w], ps[:, 0:w], AF.Exp, bias=dcol[:, e, rq, rk:rk + 1], scale=0.125)
                    cx.mm(po[:, lo:512], VA[:, kb, :].k(("VA", l, rk)), pt[:, 0:w], start=(kb == 0), stop=(kb == nkb - 1))
                    unit_hook()
                rc = rcp.get()
                cx.recip(rc[64:128, :], po[64:128, :])
                of = ofp.get()
                cx.tt(of, po[0:64, :], rc[64:128, :], ALU.mult)
                k_ = ("O2", "f", e, g)
                o2keys_f[e].append(k_)
                cx.dma(V(O2s.ap[128 + e * 64:128 + (e + 1) * 64, g * 512:(g + 1) * 512], k_), of)

            fox_load(0)
            deferred.pop(0)()
            hsteps = [(u, c) for u in range(NSEG) for c in range(16)]
            hstate = {"n": 0, "units": 0, "s1": False}

            def hgrn_step():
                n = hstate["n"]
                if n >= len(hsteps):
                    return
                if not hstate["s1"]:
                    hgrn_stage1(*hsteps[0])
                    hstate["s1"] = True
                if n + 1 < len(hsteps):
                    un, cn = hsteps[n + 1]
                    hgrn_stage1(un, cn)
                u, c = hsteps[n]
                hgrn_stage2(u, c)
                if c == 15:
                    hgrn_finish(u)
                    if u + 2 < NSEG:
                        hgrn_load(u + 2)
                hstate["n"] = n + 1

            def unit_hook():
                hstate["units"] += 1
                k = hstate["units"]
                if k >= 900 and (k - 900) % 2 == 0:
                    hgrn_step()

            for (e, g) in [(e, g) for e in range(2) for g in range(16)]:
                if e == 1 and g == 0:
                    fox_load(1)
                    deferred.pop(0)()
                    for r in range(4):
                        for w_ in range(2):
                            cx.gather(ab[:, w_, r * 32:(r + 1) * 32], ABg2, ABg.key, IDX[("ab", r, w_)])
                    hgrn_load(0)
                    hgrn_load(1)
                fox_tile(e, g)
                if g == 15:
                    chunked_gather("O2", O2g, O2s, o2keys_f[e], only=[2 + e])
                if (e == 0 and 3 <= g <= 11) or (e == 1 and g >= 2):
                    cast_next(3)
            while hstate["n"] < len(hsteps):
                hgrn_step()
            chunked_gather("O2", O2g, O2s, o2keys, only=[0, 1])

        def phase_B(l, last):
            P.barrier()
            abf.reset()
            af3.reset()
            env["f32t"] = Pool(af3.gets("f32t", [128, 512], 8))
            env["bft"] = Pool(abf.gets("bft", [128, 512], 4))
            mT = af3.get("mT", [128, 8, 256])
            hTt = abf.get("hT", [128, 8, 512])
            hT = [hTt[:, kc, :].k(("hTb", l, kc)) for kc in range(8)]
            actTt = abf.get("actT", [128, 24, 512])
            actT = [actTt[:, f, :].k(("actTb", l, f)) for f in range(24)]
            ws = WStream(cx, abf.gets("ws", [128, 2048], 3))
            wl = WStream(cx, abf.gets("wl", [128, 4096], 3))
            mnT = abf.get("mnT", [128, 8, 256])
            KcT = abf.get("KcT", [128, 8, 256])
            Vc = abf.get("Vc", [128, 2, 1024])
            ptp = Pool(abf.gets("pt", [128, 512], 4))
            oTc = [actT[kc] for kc in range(8)]
            qTc = [actT[8 + kc] for kc in range(8)]
            ocTc = [actT[16 + kc] for kc in range(8)]
            cx.dma(mT, V(min_.ap.rearrange("(k p) t -> p k t", p=128), min_.key))

            def sq(off, c):
                return wv_(l, off + c * 1024, 1024)

            rmsnorm(cx, env, lambda kc: mT[:, kc, :], lambda kc: vcol(l, 3, kc),
                    lambda kc: mnT[:, kc, :].k(("mnT", l, kc)), 256)
            bs = ws.extend([sq(B_WK, m) for m in range(8)])
            bl = wl.extend([wv_(l, B_WV + c * 4096, 4096) for c in range(2)])
            for m in range(8):
                w = ws.get(bs + m)
                wv = V(w.ap[:, 0:1024].rearrange("p (k n) -> p k n", k=8), w.key)
                p = env["psum"].get()
                for kc in range(8):
                    cx.mm(p[:, 0:256], wv[:, kc, :], mnT[:, kc, :].k(("mnT", l, kc)), start=(kc == 0), stop=(kc == 7))
                cx.copy(KcT[:, m, :].k(("KcT", l, m)), p[:, 0:256], eng="act")
            for half in range(2):
                w = wl.get(bl + half)
                wv = V(w.ap.rearrange("p (k n) -> p k n", k=8), w.key)
                for mb in range(2):
                    p = env["psum"].get()
                    for kc in range(8):
                        cx.mm(p, mnT[:, kc, mb * 128:(mb + 1) * 128].k(("mnT", l, kc)), wv[:, kc, :],
                              start=(kc == 0), stop=(kc == 7))
                    cx.copy(Vc[:, mb, half * 512:(half + 1) * 512].k(("Vc", l, mb, half)), p, eng="dve")
            for i in range(4):
                tsl = slice(i * 512, (i + 1) * 512)
                xt = lambda m: xT[:, m, tsl].k(("x", m))
                bs = ws.extend([sq(B_WOUT, m) for m in range(8)] + [sq(B_WQ, m) for m in range(8)] +
                               [sq(B_WO, m) for m in range(8)] +
                               [wv_(l, B_WGU + f * 2048, 2048) for f in range(22)])
                bl = wl.extend([wv_(l, B_WD + m * 2816, 2816) for m in range(8)])
                O2g2 = O2g.ap.rearrange("r (h c) -> (r h) c", h=16)
                for kc in range(8):
                    cx.gather(oTc[kc], O2g2, O2g.key, IDX[("o2", kc, i)])

                def sqmm(base, rhs_fn, evac):
                    for m in range(8):
                        w = ws.get(base + m)
                        wv = V(w.ap[:, 0:1024].rearrange("p (k n) -> p k n", k=8), w.key)
                        p = env["psum"].get()
                        for kc in range(8):
                            cx.mm(p, wv[:, kc, :], rhs_fn(kc), start=(kc == 0), stop=(kc == 7))
                        evac(m, p)

                cast_next(4)
                sqmm(bs, lambda kc: oTc[kc], lambda m, p: cx.tt(xt(m), p, xt(m), ALU.add))
                rmsnorm(cx, env, xt, lambda kc: vcol(l, 2, kc), lambda kc: hT[kc], 512)
                sqmm(bs + 8, lambda kc: hT[kc],
                     lambda m, p: cx.copy(qTc[m], p, eng=("act" if m % 2 else "dve")))
                for hd in range(4):
                    pts = []
                    for mb in range(2):
                        p = env["psum"].get()
                        for dc in range(2):
                            m = hd * 2 + dc
                            cx.mm(p, KcT[:, m, mb * 128:(mb + 1) * 128].k(("KcT", l, m)), qTc[m],
                                  start=(dc == 0), stop=(dc == 1))
                        pt = ptp.get()
                        cx.act(pt, p, AF.Exp, scale=1.0 / 16)
                        pts.append(pt)
                    pl = env["psum"].get()
                    for mb in range(2):
                        cx.mm(pl, env["ones"], pts[mb], start=(mb == 0), stop=(mb == 1))
                    rc = env["f32t"].get()
                    cx.recip(rc, pl)
                    for dc in range(2):
                        m = hd * 2 + dc
                        p = env["psum"].get()
                        for mb in range(2):
                            cx.mm(p, Vc[:, mb, m * 128:(m + 1) * 128].k(("Vc", l, mb, m // 4)), pts[mb],
                                  start=(mb == 0), stop=(mb == 1))
                        cx.tt(ocTc[m], p, rc, ALU.mult)
                sqmm(bs + 16, lambda kc: ocTc[kc], lambda m, p: cx.tt(xt(m), p, xt(m), ALU.add))
                rmsnorm(cx, env, xt, lambda kc: vcol(l, 4, kc), lambda kc: hT[kc], 512)
                ffn(cx, env, xt, hT, actT, ws, wl, bs + 24, bl)
                if last:
                    rstd = rmsnorm(cx, env, xt, lambda kc: vec[:, 82 + kc:83 + kc], lambda kc: hT[kc], 512)
                    for m in range(8):
                        y = env["f32t"].get()
                        cx.stt(y, xt(m), vec[:, 82 + m:83 + m], rstd, ALU.mult, ALU.mult)
                        cx.dma(V(yo.ap.rearrange("(k p) t -> p k t", p=128)[:, m, tsl], ("yo", m, i)), y, final=True)

        def dump_x():
            P.barrier()
            for i in range(4):
                tsl = slice(i * 512, (i + 1) * 512)
                for m in range(8):
                    cx.dma(V(yo.ap.rearrange("(k p) t -> p k t", p=128)[:, m, tsl], ("yo", m, i)),
                           xT[:, m, tsl].k(("x", m)), final=True)

        done = False
        for l in range(2):
            phase_A(l)
            if STOP == "A%d" % l:
                dump_x()
                break
            phase_M(l)
            if STOP == "M%d" % l:
                dump_x()
                break
            phase_B(l, l == 1 and STOP is None)
            if STOP == "B%d" % l:
                dump_x()
                break
            if l == 0:
                P.barrier()
        cx.P.emit(nc, cx.es)
    return nc


_CACHE = {}


def _sqw(w):
    return np.ascontiguousarray(w.reshape(8, 128, 8, 128).transpose(1, 2, 0, 3)).reshape(128, 8192)


def _wgu(wg, wu):
    g = wg.reshape(8, 128, 22, 128).transpose(1, 2, 0, 3)
    u = wu.reshape(8, 128, 22, 128).transpose(1, 2, 0, 3)
    return np.ascontiguousarray(np.stack([g, u], axis=2)).reshape(128, 22 * 2048)


def _wd(wd):
    return np.ascontiguousarray(wd.reshape(22, 128, 8, 128).transpose(1, 2, 0, 3)).reshape(128, 8 * 2816)


def _vec(v):
    return np.ascontiguousarray(v.reshape(8, 128).T)


def kernel(x, mem, ffn1_norm, ffn1_w_gate, ffn1_w_up, ffn1_w_down, mix_norm, w_in, hgrn_lb, hgrn_out_norm,
           fox_f_bias, w_out, cross_norm, mem_norm, cross_wq, cross_wk, cross_wv, cross_wo, ffn2_norm,
           ffn2_w_gate, ffn2_w_up, ffn2_w_down, final_norm):
    f32 = np.float32
    A = lambda a: np.asarray(a, dtype=f32)
    x = A(x)
    mem = A(mem)
    ones = np.ones((128, 128), NPBF)
    ident = np.eye(128, dtype=f32).astype(NPBF)
    jj = np.arange(128)
    maskn = np.where(jj[:, None] > jj[None, :], -30000.0, 0.0).astype(NPBF)
    ss = np.arange(64)
    tri = (ss[:, None] <= ss[None, :]).astype(f32).astype(NPBF)
    scan = np.ones((128, 512), f32)
    scan[:, 0::64] = 0.0
    ones3 = np.ones((3, 8192), NPBF)
    wpk = []
    vcols = []
    for l in range(2):
        win = A(w_in[l])
        cols = []
        for h in range(4):
            for base in (0, 512, 1536):
                cols.append(np.arange(base + h * 128, base + (h + 1) * 128))
        cols.append(np.arange(2048, 3072))
        winf = win[:, np.concatenate(cols)]
        winf = np.ascontiguousarray(winf.reshape(8, 128, 20, 128).transpose(1, 2, 0, 3)).reshape(128, 20 * 1024)
        wint = win[:, np.r_[1024:1536, 3072:3584]]
        wint = np.ascontiguousarray(wint.reshape(8, 128, 2, 512).transpose(1, 2, 0, 3)).reshape(128, 2 * 4096)
        wff = np.ascontiguousarray(win[:, 3584:3592].reshape(8, 128, 8).transpose(1, 0, 2)).reshape(128, 64)
        wv = np.ascontiguousarray(A(cross_wv[l]).reshape(8, 128, 2, 512).transpose(1, 2, 0, 3)).reshape(128, 8192)
        w = np.concatenate([_wgu(A(ffn1_w_gate[l]), A(ffn1_w_up[l])), _wd(A(ffn1_w_down[l])), winf, wint, wff,
                            _sqw(A(w_out[l])), _sqw(A(cross_wq[l])), _sqw(A(cross_wk[l])), _sqw(A(cross_wo[l])), wv,
                            _wgu(A(ffn2_w_gate[l]), A(ffn2_w_up[l])), _wd(A(ffn2_w_down[l]))], axis=1)
        assert w.shape[1] == W_COLS
        wpk.append(np.ascontiguousarray(w))
        vcols += [_vec(A(ffn1_norm[l])), _vec(A(mix_norm[l])), _vec(A(cross_norm[l])), _vec(A(mem_norm[l])),
                  _vec(A(ffn2_norm[l])), A(hgrn_out_norm[l]).reshape(128, 1)]
    vcols.append(_vec(A(final_norm)))
    vecs = np.ascontiguousarray(np.concatenate(vcols, axis=1))
    assert vecs.shape[1] == NVEC
    lb = A(hgrn_lb)
    lbp = np.ascontiguousarray(np.concatenate([lb[0].reshape(4, 128).T, lb[1].reshape(4, 128).T], axis=1))
    fbv = np.ascontiguousarray(A(fox_f_bias).T)
    if "F" not in _CACHE:
        _CACHE["F"] = build_fused()
    in_maps = []
    for c in range(8):
        b, j = c // 4, c % 4
        in_maps.append({"xT": np.ascontiguousarray(x[b, j * 2048:(j + 1) * 2048, :].T),
                        "memT": np.ascontiguousarray(mem[b].T), "w0": wpk[0], "w1": wpk[1], "vecs": vecs,
                        "lbp": lbp, "fb": fbv, "cones": ones, "cident": ident, "cmask": maskn, "ctri": tri,
                        "cscan": scan, "cones3": ones3, "idxd": _idx_table(j)})
    res = run_bass_kernel_spmd(_CACHE["F"], in_maps, core_ids=list(range(8)))
    out = np.zeros((2, 8192, 1024), f32)
    for c in range(8):
        out[c // 4, (c % 4) * 2048:(c % 4 + 1) * 2048, :] = np.asarray(res.results[c]["yo"]).T
    return out
```

```python
import contextlib
import numpy as np
import ml_dtypes
import concourse.bass as bass
import concourse.mybir as mybir
from concourse.bass_utils import run_bass_kernel_spmd

F32 = mybir.dt.float32
BF16 = mybir.dt.bfloat16
ALU = mybir.AluOpType
AF = mybir.ActivationFunctionType
NPBF = ml_dtypes.bfloat16

NDMA = 32
EPS = 1e-6


class Op:
    __slots__ = ("eng", "fn", "deps", "sig", "sigval", "is_dma", "dsem", "dval", "extra")

    def __init__(self, eng, fn, is_dma=False):
        self.eng = eng
        self.fn = fn
        self.deps = set()
        self.sig = False
        self.sigval = 0
        self.is_dma = is_dma
        self.dsem = None
        self.dval = 0


class V:
    __slots__ = ("ap", "key")

    def __init__(self, ap, key):
        self.ap = ap
        self.key = key

    def __getitem__(self, idx):
        return V(self.ap[idx], self.key)

    def k(self, key):
        return V(self.ap, key)


class Prog:
    ENGS = ("pe", "act", "dve", "pool", "sp")

    def __init__(self):
        self.q = {e: [] for e in self.ENGS}
        self.lastw = {}
        self.readers = {}
        self.dma_rr = 0
        self.dma_rr2 = 0
        self.dma_rr3 = 0
        self.dma_last = [None] * NDMA
        self.dma_cnt = [0] * NDMA
        self.out_dmas = []
        self.ncoll = 0
        self.coll_last = []
        self.pending = {}
        self.jv = {}
        self.persist = set()

    def barrier(self):
        lasts = set()
        for e in self.ENGS:
            for o in reversed(self.q[e]):
                if not o.is_dma and o.fn is not None:
                    o.sig = True
                    lasts.add(o)
                    break
        for o in self.dma_last:
            if o is not None:
                lasts.add(o)
        self.pending = {e: set(lasts) for e in self.ENGS}
        pk = lambda k: (k if isinstance(k, str) else k[0]) in self.persist
        self.lastw = {k: v for k, v in self.lastw.items() if pk(k)}
        self.readers = {k: v for k, v in self.readers.items() if pk(k)}

    def op(self, eng, fn, reads=(), writes=(), dma=False, final=False, coll=False, cast=False):
        o = Op(eng, fn, dma or coll)
        o.extra = 16
        deps = set(self.pending.pop(eng, ()))
        for r in reads:
            w = self.lastw.get(r)
            if w is not None:
                deps.add(w)
        for wk in writes:
            w = self.lastw.get(wk)
            if w is not None:
                deps.add(w)
            rd = self.readers.get(wk)
            if rd:
                deps.update(rd[0].values())
                deps.update(rd[1])
        if dma:
            if eng == "sp":
                k = self.dma_rr
                self.dma_rr = (k + 1) % 12
            elif cast:
                k = 24 + self.dma_rr3
                self.dma_rr3 = (self.dma_rr3 + 1) % 4
            else:
                k = 12 + self.dma_rr2
                self.dma_rr2 = (self.dma_rr2 + 1) % 12
            if self.dma_last[k] is not None:
                deps.add(self.dma_last[k])
            self.dma_cnt[k] += 1
            o.dsem = k
            o.dval = 16 * self.dma_cnt[k]
            self.dma_last[k] = o
        if coll:
            if self.coll_last:
                deps.add(self.coll_last[-1])
            o.dsem = NDMA
            self.ncoll += 1
            o.dval = self.ncoll
            o.extra = 1
            self.coll_last.append(o)
        for r in reads:
            rd = self.readers.setdefault(r, ({}, []))
            if dma or coll:
                rd[1].append(o)
            else:
                rd[0][eng] = o
        for wk in writes:
            self.lastw[wk] = o
            self.readers[wk] = ({}, [])
        deps.discard(o)
        if eng == "pe" and not dma:
            deps = {d for d in deps if d.is_dma or d.eng != "pe"}
        for d in deps:
            if not d.is_dma:
                d.sig = True
        o.deps = deps
        self.q[eng].append(o)
        if final:
            self.out_dmas.append(o)
        return o

    def emit(self, nc, es):
        esem = {e: es.enter_context(nc.semaphore("s_" + e)) for e in self.ENGS}
        dsem = [es.enter_context(nc.semaphore("d%d" % i)) for i in range(NDMA + 1)]
        for e in self.ENGS:
            c = 0
            for o in self.q[e]:
                if o.sig and not o.is_dma:
                    c += 1
                    o.sigval = c
        fin = Op("sp", None)
        fin.deps = set(self.out_dmas)
        self.q["sp"].append(fin)
        block = es.enter_context(nc.Block())
        prog = self

        def run(e, E):
            waited = {}
            if e == "sp":
                prog.jv["j"] = E.partition_id() % 4
            for o in prog.q[e]:
                need = {}
                for d in o.deps:
                    if d.is_dma:
                        s, v = ("d", d.dsem), d.dval
                    else:
                        s, v = ("e", d.eng), d.sigval
                    if need.get(s, 0) < v:
                        need[s] = v
                for s, v in need.items():
                    if waited.get(s, 0) >= v:
                        continue
                    waited[s] = v
                    E.wait_ge(dsem[s[1]] if s[0] == "d" else esem[s[1]], v)
                if o.fn is None:
                    continue
                ins = o.fn(E)
                if o.is_dma:
                    ins.then_inc(dsem[o.dsem], o.extra)
                elif o.sig:
                    ins.then_inc(esem[e], 1)

        @block.tensor
        def _(E):
            run("pe", E)

        @block.scalar
        def _(E):
            run("act", E)

        @block.vector
        def _(E):
            run("dve", E)

        @block.gpsimd
        def _(E):
            run("pool", E)

        @block.sync
        def _(E):
            run("sp", E)


class Ctx:
    def __init__(self):
        self.nc = bass.Bass("TRN2", target_bir_lowering=False)
        self.P = Prog()
        self.es = contextlib.ExitStack()
        self.nkey = 0

    def sb(self, name, shape, dt, key=None):
        t = self.es.enter_context(self.nc.sbuf_tensor(name, list(shape), dt))
        return V(t[:], key or name)

    def sbs(self, name, shape, dt, n):
        return [self.sb("%s%d" % (name, i), shape, dt) for i in range(n)]

    def ps(self, name, shape, dt):
        t = self.es.enter_context(self.nc.psum_tensor(name, list(shape), dt))
        return V(t[:], name)

    def dram(self, name, shape, dt, kind="Internal"):
        t = self.nc.dram_tensor(name, list(shape), dt, kind=kind)
        return V(t.ap(), name)

    def mm(self, out, lhsT, rhs, start=True, stop=True):
        r = [lhsT.key, rhs.key] + ([] if start else [out.key])
        self.P.op("pe", lambda E: E.matmul(out.ap, lhsT.ap, rhs.ap, start=start, stop=stop), r, [out.key])

    def transpose(self, out, in_, ident):
        self.P.op("pe", lambda E: E.transpose(out.ap, in_.ap, ident.ap), [in_.key, ident.key], [out.key])

    def act(self, out, in_, func, bias=None, scale=None):
        r = [in_.key]
        kw = {}
        if bias is not None:
            if isinstance(bias, V):
                r.append(bias.key)
                kw["bias"] = bias.ap
            else:
                kw["bias"] = float(bias)
        if scale is not None:
            kw["scale"] = float(scale)
        self.P.op("act", lambda E: E.activation(out=out.ap, in_=in_.ap, func=func, **kw), r, [out.key])

    def tt(self, out, in0, in1, op, eng="dve"):
        self.P.op(eng, lambda E: E.tensor_tensor(out=out.ap, in0=in0.ap, in1=in1.ap, op=op),
                  [in0.key, in1.key], [out.key])

    def ts(self, out, in0, s1, s2, op0, op1=None, eng="dve"):
        r = [in0.key]
        a1 = s1
        a2 = s2
        if isinstance(s1, V):
            r.append(s1.key)
            a1 = s1.ap
        if isinstance(s2, V):
            r.append(s2.key)
            a2 = s2.ap
        if op1 is None:
            self.P.op(eng, lambda E: E.tensor_scalar(out=out.ap, in0=in0.ap, scalar1=a1, scalar2=0.0, op0=op0,
                                                     op1=ALU.add), r, [out.key])
        else:
            self.P.op(eng, lambda E: E.tensor_scalar(out=out.ap, in0=in0.ap, scalar1=a1, scalar2=a2, op0=op0, op1=op1),
                      r, [out.key])

    def stt(self, out, in0, scalar, in1, op0, op1, eng="dve"):
        r = [in0.key, in1.key]
        a = scalar
        if isinstance(scalar, V):
            r.append(scalar.key)
            a = scalar.ap
        self.P.op(eng, lambda E: E.scalar_tensor_tensor(out=out.ap, in0=in0.ap, scalar=a, in1=in1.ap, op0=op0, op1=op1),
                  r, [out.key])

    def copy(self, out, in_, eng="dve"):
        if eng == "act":
            self.P.op("act", lambda E: E.copy(out=out.ap, in_=in_.ap), [in_.key], [out.key])
        else:
            self.P.op(eng, lambda E: E.tensor_copy(out=out.ap, in_=in_.ap), [in_.key], [out.key])

    def recip(self, out, in_):
        self.P.op("dve", lambda E: E.reciprocal(out=out.ap, in_=in_.ap), [in_.key], [out.key])

    def scan(self, out, d0, d1, initial, op0, op1):
        r = [d0.key, d1.key]
        a = initial
        if isinstance(initial, V):
            r.append(initial.key)
            a = initial.ap
        self.P.op("dve", lambda E: E.tensor_tensor_scan(out=out.ap, data0=d0.ap, data1=d1.ap, initial=a, op0=op0, op1=op1),
                  r, [out.key])

    def memset(self, out, val, eng="dve"):
        self.P.op(eng, lambda E: E.memset(out.ap, val), [], [out.key])

    def gather(self, out, src_ap, src_key, col):
        idx = self.idxt
        self.P.op("pool", lambda E: E.indirect_dma_start(
            out=out.ap, out_offset=None, in_=src_ap,
            in_offset=bass.IndirectOffsetOnAxis(ap=idx.ap[:, col:col + 1], axis=0)),
            [src_key, idx.key], [out.key], dma=True)

    def dma_dyn(self, out, in_fn, in_key):
        P = self.P
        self.P.op("sp", lambda E: E.dma_start(out=out.ap, in_=in_fn(P.jv["j"])), [in_key], [out.key], dma=True)

    def collective(self, out, in_, in_keys, groups):
        self.P.op("pool", lambda E: E.collective_compute("AllGather", ALU.bypass, replica_groups=groups,
                                                         ins=[in_.ap], outs=[out.ap]),
                  list(in_keys), [out.key], coll=True)

    def dma(self, out, in_, eng="sp", final=False, extra_reads=(), cast=False):
        self.P.op(eng, lambda E: E.dma_start(out=out.ap, in_=in_.ap), [in_.key] + list(extra_reads), [out.key],
                  dma=True, final=final, cast=cast)


class Pool:
    def __init__(self, tiles):
        self.t = tiles
        self.i = 0

    def get(self):
        v = self.t[self.i % len(self.t)]
        self.i += 1
        return v


class WStream:
    def __init__(self, cx, slots):
        self.cx = cx
        self.slots = slots
        self.items = []
        self.nxt = 0

    def extend(self, items):
        base = len(self.items)
        self.items.extend(items)
        return base

    def get(self, idx):
        n = len(self.slots)
        while self.nxt < len(self.items) and self.nxt <= idx + n - 1:
            src, w = self.items[self.nxt]
            slot = self.slots[self.nxt % n]
            self.cx.dma(slot[:, 0:w], src)
            self.nxt += 1
        return self.slots[idx % n]


def cast_weights(cx, wsrc, wdst, pieces):
    for (a, b) in pieces:
        cx.dma(wdst[:, a:b].k(("wb", a)), wsrc[:, a:b], eng="pool")


def wview(wdst, pieces, off, w):
    for (a, b) in pieces:
        if a <= off and off + w <= b:
            return wdst[:, off:off + w].k(("wb", a))
    raise AssertionError((off, w))


def rmsnorm(cx, env, src, wcols, dst, ntok, inv_n=1.0 / 1024, nk=8):
    pss = env["psum"].get()
    for kc in range(nk):
        sq = env["bft"].get()
        cx.act(sq[:, 0:ntok], src(kc), AF.Square)
        cx.mm(pss[:, 0:ntok], env["ones"], sq[:, 0:ntok], start=(kc == 0), stop=(kc == nk - 1))
    t = env["f32t"].get()
    cx.act(t[:, 0:ntok], pss[:, 0:ntok], AF.Sqrt, bias=EPS, scale=inv_n)
    rstd = env["f32t"].get()
    cx.recip(rstd[:, 0:ntok], t[:, 0:ntok])
    for kc in range(nk):
        cx.stt(dst(kc), src(kc), wcols(kc), rstd[:, 0:ntok], ALU.mult, ALU.mult)
    return rstd


def ffn(cx, env, xt, hT, actT, ws, wl, base_s, base_l):
    for f in range(22):
        w = ws.get(base_s + f)
        wv = V(w.ap.rearrange("p (g k n) -> p g k n", g=2, k=8), w.key)
        pg = env["psum"].get()
        pu = env["psum"].get()
        for kc in range(8):
            cx.mm(pg, wv[:, 0, kc, :], hT[kc], start=(kc == 0), stop=(kc == 7))
        for kc in range(8):
            cx.mm(pu, wv[:, 1, kc, :], hT[kc], start=(kc == 0), stop=(kc == 7))
        sg = env["f32t"].get()
        cx.act(sg, pg, AF.Silu)
        cx.tt(actT[f], sg, pu, ALU.mult)
    for m in range(8):
        w = wl.get(base_l + m)
        wv = V(w.ap[:, 0:2816].rearrange("p (f n) -> p f n", f=22), w.key)
        po = env["psum"].get()
        for f in range(22):
            cx.mm(po, wv[:, f, :], actT[f], start=(f == 0), stop=(f == 21))
        cx.stt(xt(m), po, 0.5, xt(m), ALU.mult, ALU.add)


def common_env(cx, nps=7, nf=10, nb=8):
    env = {}
    env["psum"] = Pool([cx.ps("ps%d" % i, [128, 512], F32) for i in range(nps)])
    env["f32t"] = Pool(cx.sbs("f32t", [128, 512], F32, nf))
    env["bft"] = Pool(cx.sbs("bft", [128, 512], BF16, nb))
    return env


def load_const(cx, name, dram, shape, dt):
    t = cx.sb(name, shape, dt)
    cx.dma(t, dram)
    return t


_GB = [0, 6, 12, 17, 22]
A_WGU = 0
A_WD = A_WGU + 22 * 2048
A_WINF = A_WD + 8 * 2816
A_WINT = A_WINF + 20 * 1024
A_WFF = A_WINT + 2 * 4096
WA_COLS = A_WFF + 64
B0 = WA_COLS
B_WOUT = B0
B_WQ = B_WOUT + 8192
B_WK = B_WQ + 8192
B_WO = B_WK + 8192
B_WV = B_WO + 8192
B_WGU = B_WV + 8192
B_WD = B_WGU + 22 * 2048
W_COLS = B_WD + 8 * 2816
def _pieces(base, nchunk, width, per):
    return [(base + c * width, base + min(c + per, nchunk) * width) for c in range(0, nchunk, per)]


_wf = _pieces(A_WINF, 20, 1024, 4)
PIECES_A = (_pieces(A_WGU, 22, 2048, 2) + _pieces(A_WD, 8, 2816, 1) + _wf[3:] + [(A_WINT + 4096, WA_COLS)] +
            _wf[:3] + [(A_WINT, A_WINT + 4096)])
PIECES_B = (_pieces(B_WK, 8, 1024, 4) + _pieces(B_WV, 2, 4096, 1) + _pieces(B_WOUT, 8, 1024, 4) +
            _pieces(B_WQ, 8, 1024, 4) + _pieces(B_WO, 8, 1024, 4) + _pieces(B_WGU, 22, 2048, 2) +
            _pieces(B_WD, 8, 2816, 1))
PIECES = PIECES_A + PIECES_B
GROUPS = [[0, 1, 2, 3], [4, 5, 6, 7]]
STOP = None
NOCOLL = False
RCH = {"HG": 256, "VH": 128, "QA": 140, "KA": 140, "VF": 256, "AB": 512, "TO": 8, "O2": 64}


def _grow(row, r, rc):
    return (row // rc) * 4 * rc + r * rc + row % rc
NVEC = 2 * 41 + 8
U32 = mybir.dt.uint32


def _idx_names():
    n = []
    n += [("hg", r, k, half) for r in range(4) for k in range(4) for half in range(2)]
    n += [("vh", r, half) for r in range(4) for half in range(2)]
    n += [("ab", r, w) for r in range(4) for w in range(2)]
    n += [("tot", r, e) for r in range(4) for e in range(2)]
    n += [("qa", r, e) for r in range(4) for e in range(2)]
    n += [("va", r, e) for r in range(4) for e in range(2)]
    n += [("o2", kc, i) for kc in range(8) for i in range(4)]
    return n


IDX = {nm: i for i, nm in enumerate(_idx_names())}
NIDX = len(IDX)


def _idx_table(j):
    p = np.arange(128)
    t = np.zeros((128, NIDX), np.uint32)
    for nm, col in IDX.items():
        if nm[0] == "hg":
            _, r, k, half = nm
            v = 2 * _grow(j * 512 + k * 128 + p, r, RCH["HG"]) + half
        elif nm[0] == "vh":
            _, r, half = nm
            v = 2 * _grow(j * 64 + (p % 64), r, RCH["VH"]) + half
        elif nm[0] == "ab":
            _, r, w = nm
            v = 2 * (r * 512 + j * 128 + p) + w
        elif nm[0] == "tot":
            _, r, e = nm
            v = np.full(128, r * 8 + 2 * j + e)
        elif nm[0] == "qa":
            _, r, e = nm
            v = _grow((2 * j + e) * 70 + np.minimum(p, 69), r, RCH["QA"])
        elif nm[0] == "va":
            _, r, e = nm
            v = _grow((2 * j + e) * 128 + p, r, RCH["VF"])
        else:
            _, kc, i = nm
            v = 16 * _grow((kc // 4) * 128 + p, kc % 4, RCH["O2"]) + j * 4 + i
        t[:, col] = v
    return t


class Arena:
    def __init__(self, cx, name, n, dt):
        self.t = cx.sb(name, [128, n], dt)
        self.n = n
        self.off = 0
        self.gen = 0

    def reset(self):
        self.off = 0
        self.gen += 1

    def get(self, name, shape):
        size = 1
        for d in shape[1:]:
            size *= d
        ap = self.t.ap[0:shape[0], self.off:self.off + size]
        self.off += (size + 15) // 16 * 16
        assert self.off <= self.n, (name, self.off, self.n)
        if len(shape) == 3:
            ap = ap.rearrange("p (a b) -> p a b", a=shape[1])
        elif len(shape) == 4:
            ap = ap.rearrange("p (a b c) -> p a b c", a=shape[1], b=shape[2])
        return V(ap, (name, self.gen))

    def gets(self, name, shape, n):
        return [self.get("%s%d" % (name, i), shape) for i in range(n)]


def build_fused():
    cx = Ctx()
    nc = cx.nc
    xin = cx.dram("xT", [1024, 2048], F32, "ExternalInput")
    min_ = cx.dram("memT", [1024, 256], F32, "ExternalInput")
    wsrc = [cx.dram("w%d" % l, [128, W_COLS], F32, "ExternalInput") for l in range(2)]
    vecs = cx.dram("vecs", [128, NVEC], F32, "ExternalInput")
    lbp = cx.dram("lbp", [128, 8], F32, "ExternalInput")
    fb = cx.dram("fb", [8, 2], F32, "ExternalInput")
    cones = cx.dram("cones", [128, 128], BF16, "ExternalInput")
    cident = cx.dram("cident", [128, 128], BF16, "ExternalInput")
    cmask = cx.dram("cmask", [128, 128], BF16, "ExternalInput")
    ctri = cx.dram("ctri", [64, 64], BF16, "ExternalInput")
    cscan = cx.dram("cscan", [128, 512], F32, "ExternalInput")
    cones3 = cx.dram("cones3", [3, 8192], BF16, "ExternalInput")
    idxd = cx.dram("idxd", [128, NIDX], U32, "ExternalInput")
    yo = cx.dram("yo", [1024, 2048], F32, "ExternalOutput")
    wb = [cx.dram("wb%d" % l, [128, W_COLS], BF16) for l in range(2)]
    HGs = cx.dram("HGs", [2048, 2048], BF16)
    HGg = cx.dram("HGg", [4 * 2048, 2048], BF16)
    VHs = cx.dram("VHs", [256, 4096], BF16)
    VHg = cx.dram("VHg", [4 * 256, 4096], BF16)
    QAs = cx.dram("QAs", [560, 2048], BF16)
    QAg = cx.dram("QAg", [4 * 560, 2048], BF16)
    KAs = cx.dram("KAs", [560, 2048], BF16)
    KAg = cx.dram("KAg", [4 * 560, 2048], BF16)
    VFs = cx.dram("VFs", [1024, 2048], BF16)
    VFg = cx.dram("VFg", [4 * 1024, 2048], BF16)
    ABs = cx.dram("ABs", [512, 64], F32)
    ABg = cx.dram("ABg", [4 * 512, 64], F32)
    TOs = cx.dram("TOs", [8, 16], F32)
    TOg = cx.dram("TOg", [4 * 8, 16], F32)
    O2s = cx.dram("O2s", [256, 8192], BF16)
    O2g = cx.dram("O2g", [4 * 256, 8192], BF16)

    with cx.es:
        P = cx.P
        P.persist = {"HG", "VH", "QA", "KA", "VF", "AB", "O2", "QA1", "KA1", "wb", "yo", "HGg", "VHg", "QAg", "KAg",
                     "VFg", "ABg", "O2g", "TO", "TOs", "TOg", "HGs", "VHs", "QAs", "KAs", "VFs", "ABs", "O2s", "w0", "w1"}
        env = {}
        psl = [cx.ps("ps%d" % i, [128, 512], F32) for i in range(7)]
        pTR = cx.ps("pTR", [128, 1024], BF16)
        env["psum"] = Pool(psl)
        castq = [(l, a, b) for l in range(2) for (a, b) in PIECES]
        castpos = [0]

        def cast_next(n):
            for _ in range(n):
                if castpos[0] < len(castq):
                    l_, a, b = castq[castpos[0]]
                    castpos[0] += 1
                    xr = [("x", kc) for kc in range(8)] if first_cast[0] else []
                    first_cast[0] = False
                    cx.dma(wb[l_][:, a:b].k(("wb", l_, a)), wsrc[l_][:, a:b], eng="pool", cast=True, extra_reads=xr)

        first_cast = [True]

        def wv_(l, off, w):
            for (a, b) in PIECES:
                if a <= off and off + w <= b:
                    return (wb[l][:, off:off + w].k(("wb", l, a)), w)
            raise AssertionError((off, w))

        xT = cx.sb("xTs", [128, 8, 2048], F32)
        xin_v = V(xin.ap.rearrange("(k p) t -> p k t", p=128), xin.key)
        for kc in range(8):
            cx.dma(xT[:, kc, :].k(("x", kc)), xin_v[:, kc, :])
        cast_next(len(PIECES_A))
        env["ones"] = load_const(cx, "ones", cones, [128, 128], BF16)
        ident = load_const(cx, "ident", cident, [128, 128], BF16)
        maskn = load_const(cx, "maskn", cmask, [128, 128], BF16)
        tri = load_const(cx, "tri", ctri, [64, 64], BF16)
        scanm = load_const(cx, "scanm", cscan, [128, 512], F32)
        vec = load_const(cx, "vec", vecs, [128, NVEC], F32)
        lb_in = load_const(cx, "lb_in", lbp, [128, 8], F32)
        fbt = load_const(cx, "fbt", fb, [8, 2], F32)
        lbd = cx.sb("lbd", [128, 4], F32)
        cx.tt(lbd, lb_in[:, 4:8], lb_in[:, 0:4], ALU.subtract)
        lbv = cx.sb("lbv", [128, 2, 4], F32)
        cx.memset(lbv, 0.0)
        cx.act(lbv[:, 1, :], lbd, AF.Sigmoid)
        oml = cx.sb("oml", [128, 2, 4], F32)
        cx.ts(oml, lbv, -1.0, 1.0, ALU.mult, ALU.add)
        nfb = cx.sb("nfb", [8, 2], F32)
        cx.ts(nfb, fbt, -1.0, None, ALU.mult)
        onesf = cx.sb("onesf", [1, 128], F32)
        cx.memset(onesf, 1.0)
        allone = cx.sb("allone", [8, 512], F32)
        cx.memset(allone, 1.0)
        vtl = [cx.sb("vtl%d" % i, [128, 8, 128], BF16) for i in range(2)]
        for t in vtl:
            cx.memset(t, 1.0)
        idxt = load_const(cx, "idxt", idxd, [128, NIDX], U32)
        cx.idxt = idxt
        QAv = QAs.ap.rearrange("(h s) t -> h s t", s=70)
        KAv = KAs.ap.rearrange("(h s) t -> h s t", s=70)
        for hh in range(8):
            cx.dma(V(QAv[hh, 67:70, :], ("QA1", hh)), cones3[:, 0:2048])
            cx.dma(V(KAv[hh, 64:67, :], ("KA1", hh)), cones3[:, 0:2048])
        abf = Arena(cx, "abf", 42240, BF16)
        af3 = Arena(cx, "af3", 6400, F32)
        xo_keys = []
        deferred = []

        def chunked_gather(nm, g_, s_, keys_, only=None):
            rc = RCH[nm]
            rows = s_.ap.shape[0]
            for c in (range(rows // rc) if only is None else only):
                cx.collective(V(g_.ap[c * 4 * rc:(c + 1) * 4 * rc, :], g_.key), V(s_.ap[c * rc:(c + 1) * rc, :], s_.key),
                              keys_, GROUPS)

        def vcol(l, which, kc):
            c = l * 41 + which * 8 + kc
            return vec[:, c:c + 1]

        def phase_A(l):
            abf.reset()
            af3.reset()
            env["f32t"] = Pool(af3.gets("f32t", [128, 512], 10))
            env["bft"] = Pool(abf.gets("bft", [128, 512], 8))
            hTt = abf.get("hT", [128, 8, 512])
            hT = [hTt[:, kc, :].k(("hT", l, kc)) for kc in range(8)]
            actTt = abf.get("actT", [128, 22, 512])
            actT = [actTt[:, f, :].k(("actT", l, f)) for f in range(22)]
            ws = WStream(cx, abf.gets("ws", [128, 2048], 3))
            wl = WStream(cx, abf.gets("wl", [128, 4096], 2))
            cbt = Pool(abf.gets("cbt", [8, 512], 8))
            wff = abf.get("wff", [128, 64])
            abt = af3.get("abt", [128, 4, 64])
            ngt = Pool(af3.gets("ngt", [128, 8], 2))
            ccar = af3.get("ccar", [8, 4])
            tot = af3.get("tot", [8, 16])
            wffv = V(wff.ap.rearrange("p (k n) -> p k n", k=8), wff.key)
            keys = {"HG": [], "VH": [], "QA": [("QA1", hh) for hh in range(8)],
                    "KA": [("KA1", hh) for hh in range(8)], "VF": [], "AB": []}
            VHv = VHs.ap.rearrange("(h s) (c v) -> s c h v", s=64, v=128)
            VFv = VFs.ap.rearrange("(h p) (b d) -> p b h d", p=128, d=128)
            for i in range(4):
                tsl = slice(i * 512, (i + 1) * 512)
                xt = lambda m: xT[:, m, tsl].k(("x", m))
                bs = ws.extend([wv_(l, A_WGU + f * 2048, 2048) for f in range(22)] +
                               [wv_(l, A_WINF + c * 1024, 1024) for c in range(12, 20)])
                bl = wl.extend([wv_(l, A_WD + m * 2816, 2816) for m in range(8)] +
                               [wv_(l, A_WINT + 4096, 4096)])
                rmsnorm(cx, env, xt, lambda kc: vcol(l, 0, kc), lambda kc: hT[kc], 512)
                ffn(cx, env, xt, hT, actT, ws, wl, bs, bl)
                rmsnorm(cx, env, xt, lambda kc: vcol(l, 1, kc), lambda kc: hT[kc], 512)
                pc = [0]

                def proj(M=128):
                    w = ws.get(bs + 22 + pc[0])
                    pc[0] += 1
                    wv = V(w.ap[:, 0:1024].rearrange("p (k n) -> p k n", k=8), w.key)
                    p = env["psum"].get()
                    for kc in range(8):
                        cx.mm(p[0:M, :], wv[:, kc, 0:M], hT[kc], start=(kc == 0), stop=(kc == 7))
                    return p

                for qk in range(2):
                    for c in range(4):
                        p = proj()
                        t = env["bft"].get()
                        cx.copy(t, p, eng="act")
                        dstv = QAv if qk == 0 else KAv
                        nm = "QA" if qk == 0 else "KA"
                        for e_ in range(2):
                            k_ = (nm, "qk", c, e_, i)
                            keys[nm].append(k_)
                            cx.dma(V(dstv[2 * c + e_, 0:64, tsl], k_), t[e_ * 64:(e_ + 1) * 64, :])
                if i == 0:
                    cx.dma(wff, wv_(l, A_WFF, 64)[0])
                pff = env["psum"].get()
                for kc in range(8):
                    cx.mm(pff[0:8, :], wffv[:, kc, :], hT[kc], start=(kc == 0), stop=(kc == 7))
                g8 = lambda: env["f32t"].get()[0:8, :]
                ee = g8()
                cx.act(ee, pff[0:8, :], AF.Exp, bias=nfb[:, l:l + 1], scale=-1.0)
                cx.ts(ee, ee, 1.0, None, ALU.add)
                sp = g8()
                cx.act(sp, ee, AF.Ln)
                cp = g8()
                if i == 0:
                    cx.scan(cp, allone, sp, 0.0, ALU.mult, ALU.add)
                else:
                    cx.scan(cp, allone, sp, ccar[:, i - 1:i], ALU.mult, ALU.add)
                cx.copy(ccar[:, i:i + 1], cp[:, 511:512])
                c8 = g8()
                cx.ts(c8, cp, -8.0, None, ALU.mult)
                parts = []
                cur = c8
                for part in range(3):
                    hb = cbt.get()
                    cx.copy(hb, cur)
                    parts.append(hb)
                    if part < 2:
                        nxt = g8()
                        cx.tt(nxt, cur, hb, ALU.subtract)
                        cur = nxt
                for part in range(3):
                    nb = cbt.get()
                    cx.ts(nb, parts[part], -1.0, None, ALU.mult)
                    parts.append(nb)
                for part in range(6):
                    nm = "QA" if part < 3 else "KA"
                    k_ = (nm, "c", part, i)
                    keys[nm].append(k_)
                    if part < 3:
                        cx.dma(V(QAv[:, 64 + part, tsl], k_), parts[part])
                    else:
                        cx.dma(V(KAv[:, 67 + part - 3, tsl], k_), parts[part])
                for which in (1,):
                    w = wl.get(bl + 8)
                    wv = V(w.ap.rearrange("p (k n) -> p k n", k=8), w.key)
                    for blk in range(4):
                        p = env["psum"].get()
                        for kc in range(8):
                            cx.mm(p, hT[kc][:, blk * 128:(blk + 1) * 128], wv[:, kc, :], start=(kc == 0), stop=(kc == 7))
                        gb = i * 4 + blk
                        if which == 0:
                            t = env["bft"].get()
                            cx.copy(t, p, eng=("act" if blk % 2 == 0 else "dve"))
                            tv = V(t.ap.rearrange("p (h v) -> p h v", h=4), t.key)
                            for half in range(2):
                                k_ = ("VH", gb, half)
                                keys["VH"].append(k_)
                                cx.dma(V(VHv[:, gb * 2 + half, :, :], k_), tv[half * 64:(half + 1) * 64, :, :])
                        else:
                            t = vtl[blk % 2]
                            pv = V(p.ap.rearrange("p (h d) -> p h d", h=8), p.key)
                            cx.copy(t[:, :, 0:64], pv, eng=("act" if blk % 2 == 0 else "dve"))
                            k_ = ("VF", gb)
                            keys["VF"].append(k_)
                            cx.dma(V(VFv[:, gb, :, :], k_), t)
            cx.memset(tot, 0.0)
            cx.ts(tot[:, 0:1], ccar[:, 3:4], -1.0, None, ALU.mult)
            keys["TO"] = [("TO", "tot")]
            cx.dma(V(TOs.ap[0:8, 0:16], ("TO", "tot")), tot)
            for nm, s_, g_ in (("TO", TOs, TOg), ("QA", QAs, QAg), ("KA", KAs, KAg), ("VF", VFs, VFg)):
                chunked_gather(nm, g_, s_, keys[nm])
            for i in range(4):
                tsl = slice(i * 512, (i + 1) * 512)
                xt = lambda m: xT[:, m, tsl].k(("x", m))
                bs = ws.extend([wv_(l, A_WINF + c * 1024, 1024) for c in range(12)]) - 22
                bl = wl.extend([wv_(l, A_WINT, 4096)]) - 8
                rmsnorm(cx, env, xt, lambda kc: vcol(l, 1, kc), lambda kc: hT[kc], 512)
                pass
                pc = [0]

                def proj(M=128):
                    w = ws.get(bs + 22 + pc[0])
                    pc[0] += 1
                    wv = V(w.ap[:, 0:1024].rearrange("p (k n) -> p k n", k=8), w.key)
                    p = env["psum"].get()
                    for kc in range(8):
                        cx.mm(p[0:M, :], wv[:, kc, 0:M], hT[kc], start=(kc == 0), stop=(kc == 7))
                    return p

                for h in range(4):
                    pq = proj()
                    pf = proj()
                    pgt = proj()
                    sg = env["f32t"].get()
                    cx.act(sg, pf, AF.Sigmoid)
                    f = env["f32t"].get()
                    cx.ts(f, sg, oml[:, l, h:h + 1], lbv[:, l, h:h + 1], ALU.mult, ALU.add)
                    lf = env["f32t"].get()
                    cx.act(lf, f, AF.Ln)
                    kk = env["f32t"].get()
                    cx.ts(kk, f, -1.0, 1.0, ALU.mult, ALU.add)
                    g = env["f32t"].get()
                    cx.scan(g, scanm, lf, 0.0, ALU.mult, ALU.add)
                    ng = ngt.get()
                    gv = V(g.ap.rearrange("p (c t) -> p c t", t=64), g.key)
                    cx.ts(ng, gv[:, :, 31], -1.0, None, ALU.mult)
                    dd = env["f32t"].get()
                    for c in range(8):
                        cx.ts(dd[:, c * 64:(c + 1) * 64], g[:, c * 64:(c + 1) * 64], ng[:, c:c + 1], None, ALU.add)
                    e1 = env["f32t"].get()
                    cx.act(e1, dd, AF.Exp)
                    e2 = env["f32t"].get()
                    cx.act(e2, dd, AF.Exp, scale=-1.0)
                    e3 = env["f32t"].get()
                    cx.act(e3, g, AF.Exp)
                    q = env["f32t"].get()
                    cx.act(q, pq, AF.Silu)
                    qg = env["bft"].get()
                    cx.tt(qg, q, e1, ALU.mult)
                    kg = env["bft"].get()
                    cx.tt(kg, kk, e2, ALU.mult)
                    qs = env["bft"].get()
                    cx.tt(qs, q, e3, ALU.mult)
                    gt = env["bft"].get()
                    cx.act(gt, pgt, AF.Silu)
                    e3v = V(e3.ap.rearrange("p (c t) -> p c t", t=64), e3.key)
                    e1v = V(e1.ap.rearrange("p (c t) -> p c t", t=64), e1.key)
                    cx.copy(abt[:, h, i * 8:(i + 1) * 8], e3v[:, :, 63])
                    cx.copy(abt[:, h, 32 + i * 8:32 + (i + 1) * 8], e1v[:, :, 63])
                    for kind, t in enumerate((qg, kg, qs, gt)):
                        r0 = h * 512 + kind * 128
                        k_ = ("HG", h, kind, i)
                        keys["HG"].append(k_)
                        cx.dma(V(HGs.ap[r0:r0 + 128, tsl], k_), t)
                for which in (0,):
                    w = wl.get(bl + 8)
                    wv = V(w.ap.rearrange("p (k n) -> p k n", k=8), w.key)
                    for blk in range(4):
                        p = env["psum"].get()
                        for kc in range(8):
                            cx.mm(p, hT[kc][:, blk * 128:(blk + 1) * 128], wv[:, kc, :], start=(kc == 0), stop=(kc == 7))
                        gb = i * 4 + blk
                        if which == 0:
                            t = env["bft"].get()
                            cx.copy(t, p, eng=("act" if blk % 2 == 0 else "dve"))
                            tv = V(t.ap.rearrange("p (h v) -> p h v", h=4), t.key)
                            for half in range(2):
                                k_ = ("VH", gb, half)
                                keys["VH"].append(k_)
                                cx.dma(V(VHv[:, gb * 2 + half, :, :], k_), tv[half * 64:(half + 1) * 64, :, :])
                        else:
                            t = vtl[blk % 2]
                            pv = V(p.ap.rearrange("p (h d) -> p h d", h=8), p.key)
                            cx.copy(t[:, :, 0:64], pv, eng=("act" if blk % 2 == 0 else "dve"))
                            k_ = ("VF", gb)
                            keys["VF"].append(k_)
                            cx.dma(V(VFv[:, gb, :, :], k_), t)
            for h in range(4):
                k_ = ("AB", h)
                keys["AB"].append(k_)
                cx.dma(V(ABs.ap[h * 128:(h + 1) * 128, :], k_), abt[:, h, :])
            deferred.append(lambda: [chunked_gather("AB", ABg, ABs, keys["AB"]),
                                     chunked_gather("HG", HGg, HGs, keys["HG"], only=range(0, 4))])
            deferred.append(lambda: [chunked_gather("HG", HGg, HGs, keys["HG"], only=range(4, 8)),
                                     chunked_gather("VH", VHg, VHs, keys["VH"])])

        def phase_M(l):
            P.barrier()
            abf.reset()
            af3.reset()
            pS = [psl[0], psl[1], psl[3]]
            pO = [psl[2], psl[4]]
            pAT, pHO, pSU = psl[6], psl[5], psl[6]
            env["f32t"] = Pool(af3.gets("f32t", [128, 512], 6))
            env["bft"] = Pool(abf.gets("bft", [128, 512], 4))
            ab = af3.get("ab", [128, 2, 128])
            tg = af3.get("tg", [128, 8, 16])
            dcol = af3.get("dcol", [128, 2, 4, 4])
            S = af3.get("S", [128, 128])
            St = af3.get("St", [128, 128])
            ohat = af3.get("ohat", [128, 1024])
            rcp = Pool(af3.gets("rc", [128, 512], 2))
            Sb = abf.get("Sb", [128, 128])
            hslots = [[abf.get("hg%d_%d" % (k, s), [128, 1024]) for k in range(4)] for s in range(2)]
            vslots = [abf.get("vc%d" % s, [128, 2048]) for s in range(2)]
            QA = abf.get("QA", [128, 8192])
            KA = abf.get("KA", [128, 8192])
            VA = abf.get("VA", [128, 64, 128])
            ptp = Pool(abf.gets("pt", [128, 512], 3))
            ofp = Pool(abf.gets("ofs", [64, 512], 2))
            onw = vec[:, l * 41 + 40:l * 41 + 41]
            ABg2 = ABg.ap.rearrange("r (h c) -> (r h) c", h=2)
            for r in range(4):
                for e in range(2):
                    cx.gather(tg[:, r * 2 + e, :], TOg.ap, TOg.key, IDX[("tot", r, e)])
            cx.memset(dcol, 0.0)
            for e in range(2):
                for rq in range(4):
                    for rk in range(rq - 1, -1, -1):
                        cx.tt(dcol[:, e, rq, rk:rk + 1], dcol[:, e, rq, rk + 1:rk + 2],
                              tg[:, rk * 2 + e, 0:1], ALU.add)
            cx.memset(S, 0.0)
            cx.memset(Sb, 0.0)
            NSEG = 8
            o2keys = []
            o2keys_f = [[], []]

            def hgrn_load(u):
                s = u % 2
                r, half = u // 2, u % 2
                csl = slice(half * 1024, (half + 1) * 1024)
                HGg2 = HGg.ap.rearrange("r (h c) -> (r h) c", h=2)
                VHg2 = VHg.ap.rearrange("r (h c) -> (r h) c", h=2)
                for k in range(4):
                    cx.gather(hslots[s][k], HGg2, HGg.key, IDX[("hg", r, k, half)])
                cx.gather(vslots[s], VHg2, VHg.key, IDX[("vh", r, half)])

            AT2 = [abf.get("ATs%d" % i_, [64, 64]) for i_ in range(2)]
            KT2 = [abf.get("KgT%d" % i_, [64, 128]) for i_ in range(2)]
            for t_ in AT2:
                cx.memset(t_, 0.0)

            def hgrn_stage1(u, c):
                s = u % 2
                Qg, Kg, Qs, _ = hslots[s]
                t0 = c * 64
                b_ = c % 2
                pa = pAT[:, 0:64]
                ptr = pTR[:, 0:128]
                cx.mm(pa[0:64, 32:64], Kg[:, t0:t0 + 64], Qg[:, t0 + 32:t0 + 64])
                cx.mm(pa[0:32, 0:32], Kg[:, t0:t0 + 32], Qg[:, t0:t0 + 32])
                cx.tt(AT2[b_][0:64, 32:64], pa[0:64, 32:64], tri[0:64, 32:64], ALU.mult)
                cx.tt(AT2[b_][0:32, 0:32], pa[0:32, 0:32], tri[0:32, 0:32], ALU.mult)
                cx.transpose(ptr[0:64, :], Kg[:, t0:t0 + 64], ident)
                cx.copy(KT2[b_], ptr[0:64, :], eng="dve")

            def hgrn_stage2(u, c):
                s = u % 2
                Qg, Kg, Qs, _ = hslots[s]
                Vc = V(vslots[s].ap[0:64, :].rearrange("p (c v) -> p c v", v=128), vslots[s].key)
                t0 = c * 64
                cc = u * 16 + c
                b_ = c % 2
                cx.mm(pHO[:, 0:64], Sb, Qs[:, t0:t0 + 64], start=True, stop=False)
                cx.mm(pHO[:, 0:64], Vc[:, c, :], AT2[b_], start=False, stop=True)
                cx.copy(ohat[:, t0:t0 + 64], pHO[:, 0:64], eng="dve")
                cx.mm(pSU[:, 0:128], KT2[b_], Vc[:, c, :])
                cx.ts(St, S, ab[:, 0, cc:cc + 1], None, ALU.mult)
                cx.stt(S, pSU[:, 0:128], ab[:, 1, cc:cc + 1], St, ALU.mult, ALU.add)
                cx.copy(Sb, S, eng="dve")

            def hgrn_finish(u):
                s = u % 2
                gate = hslots[s][3]
                for pc_ in range(2):
                    sl = slice(pc_ * 512, (pc_ + 1) * 512)
                    sq = env["bft"].get()
                    cx.tt(sq, ohat[:, sl], ohat[:, sl], ALU.mult)
                    cx.mm(pSU, env["ones"], sq)
                    t = env["f32t"].get()
                    cx.act(t, pSU, AF.Ln, bias=EPS, scale=1.0 / 128)
                    rstd = env["f32t"].get()
                    cx.act(rstd, t, AF.Exp, scale=-0.5)
                    on = env["f32t"].get()
                    cx.stt(on, ohat[:, sl], onw, rstd, ALU.mult, ALU.mult)
                    og = env["bft"].get()
                    cx.tt(og, on, gate[:, sl], ALU.mult)
                    c0 = u * 1024 + pc_ * 512
                    k_ = ("O2", "h", u, pc_)
                    o2keys.append(k_)
                    cx.dma(V(O2s.ap[0:128, c0:c0 + 512], k_), og)

            def fox_load(e):
                for r in range(4):
                    sl = slice(r * 2048, (r + 1) * 2048)
                    cx.gather(QA[:, sl].k(("QA", l, r)), QAg.ap, QAg.key, IDX[("qa", r, e)])
                    cx.gather(KA[:, sl].k(("KA", l, r)), KAg.ap, KAg.key, IDX[("qa", r, e)])
                    vdst = V(VA.ap[:, r * 16:(r + 1) * 16, :].rearrange("p b d -> p (b d)"), ("VA", l, r))
                    cx.gather(vdst, VFg.ap, VFg.key, IDX[("va", r, e)])

            def fox_tile(e, g):
                rq = g // 4
                po = pO[g % 2]
                nkb = 4 * g + 4

                def geom(kb):
                    diag = kb >= 4 * g
                    lo = (kb - 4 * g) * 128 if diag else 0
                    return diag, lo, 512 - lo, g * 512 + lo

                def s_stage(kb):
                    rk = kb // 16
                    diag, lo, w, q0 = geom(kb)
                    ps = pS[kb % 3]
                    rdk = [("KA", l, rk), ("QA", l, rq)]
                    Kap = KA[0:70, kb * 128:(kb + 1) * 128]
                    Qap = QA[0:70, q0:q0 + w]
                    P.op("pe", lambda E, ps=ps, Kap=Kap, Qap=Qap, w=w, diag=diag:
                         E.matmul(ps.ap[:, 0:w], Kap.ap, Qap.ap, start=True, stop=not diag), rdk, [ps.key])
                    if diag:
                        cx.mm(ps[:, 0:128], ident, maskn, start=False, stop=True)

                s_stage(0)
                s_stage(1)
                for kb in range(nkb):
                    if kb + 2 < nkb:
                        s_stage(kb + 2)
                    rk = kb // 16
                    diag, lo, w, q0 = geom(kb)
                    ps = pS[kb % 3]
                    pt = ptp.get()
                    cx.act(pt[:, 0:w], ps[:, 0:w], AF.Exp, bias=dcol[:, e, rq, rk:rk + 1], scale=0.125)
                    cx.mm(po[:, lo:512], VA[:, kb, :].k(("VA", l, rk)), pt[:, 0:w], start=(kb == 0), stop=(kb == nkb - 1))
                    unit_hook()
                rc = rcp.get()
                cx.recip(rc[64:128, :], po[64:128, :])
                of = ofp.get()
                cx.tt(of, po[0:64, :], rc[64:128, :], ALU.mult)
                k_ = ("O2", "f", e, g)
                o2keys_f[e].append(k_)
                cx.dma(V(O2s.ap[128 + e * 64:128 + (e + 1) * 64, g * 512:(g + 1) * 512], k_), of)

            fox_load(0)
            deferred.pop(0)()
            hsteps = [(u, c) for u in range(NSEG) for c in range(16)]
            hstate = {"n": 0, "units": 0, "s1": False}

            def hgrn_step():
                n = hstate["n"]
                if n >= len(hsteps):
                    return
                if not hstate["s1"]:
                    hgrn_stage1(*hsteps[0])
                    hstate["s1"] = True
                if n + 1 < len(hsteps):
                    un, cn = hsteps[n + 1]
                    hgrn_stage1(un, cn)
                u, c = hsteps[n]
                hgrn_stage2(u, c)
                if c == 15:
                    hgrn_finish(u)
                    if u + 2 < NSEG:
                        hgrn_load(u + 2)
                hstate["n"] = n + 1

            def unit_hook():
                hstate["units"] += 1
                k = hstate["units"]
                if k >= 900 and (k - 900) % 2 == 0:
                    hgrn_step()

            for (e, g) in [(e, g) for e in range(2) for g in range(16)]:
                if e == 1 and g == 0:
                    fox_load(1)
                    deferred.pop(0)()
                    for r in range(4):
                        for w_ in range(2):
                            cx.gather(ab[:, w_, r * 32:(r + 1) * 32], ABg2, ABg.key, IDX[("ab", r, w_)])
                    hgrn_load(0)
                    hgrn_load(1)
                fox_tile(e, g)
                if g == 15:
                    chunked_gather("O2", O2g, O2s, o2keys_f[e], only=[2 + e])
                if (e == 0 and 3 <= g <= 11) or (e == 1 and g >= 2):
                    cast_next(3)
            while hstate["n"] < len(hsteps):
                hgrn_step()
            chunked_gather("O2", O2g, O2s, o2keys, only=[0, 1])

        def phase_B(l, last):
            P.barrier()
            abf.reset()
            af3.reset()
            env["f32t"] = Pool(af3.gets("f32t", [128, 512], 8))
            env["bft"] = Pool(abf.gets("bft", [128, 512], 4))
            mT = af3.get("mT", [128, 8, 256])
            hTt = abf.get("hT", [128, 8, 512])
            hT = [hTt[:, kc, :].k(("hTb", l, kc)) for kc in range(8)]
            actTt = abf.get("actT", [128, 24, 512])
            actT = [actTt[:, f, :].k(("actTb", l, f)) for f in range(24)]
            ws = WStream(cx, abf.gets("ws", [128, 2048], 3))
            wl = WStream(cx, abf.gets("wl", [128, 4096], 2))
            mnT = abf.get("mnT", [128, 8, 256])
            KcT = abf.get("KcT", [128, 8, 256])
            Vc = abf.get("Vc", [128, 2, 1024])
            ptp = Pool(abf.gets("pt", [128, 512], 4))
            oTc = [actT[kc] for kc in range(8)]
            qTc = [actT[8 + kc] for kc in range(8)]
            ocTc = [actT[16 + kc] for kc in range(8)]
            cx.dma(mT, V(min_.ap.rearrange("(k p) t -> p k t", p=128), min_.key))

            def sq(off, c):
                return wv_(l, off + c * 1024, 1024)

            rmsnorm(cx, env, lambda kc: mT[:, kc, :], lambda kc: vcol(l, 3, kc),
                    lambda kc: mnT[:, kc, :].k(("mnT", l, kc)), 256)
            bs = ws.extend([sq(B_WK, m) for m in range(8)])
            bl = wl.extend([wv_(l, B_WV + c * 4096, 4096) for c in range(2)])
            for m in range(8):
                w = ws.get(bs + m)
                wv = V(w.ap[:, 0:1024].rearrange("p (k n) -> p k n", k=8), w.key)
                p = env["psum"].get()
                for kc in range(8):
                    cx.mm(p[:, 0:256], wv[:, kc, :], mnT[:, kc, :].k(("mnT", l, kc)), start=(kc == 0), stop=(kc == 7))
                cx.copy(KcT[:, m, :].k(("KcT", l, m)), p[:, 0:256], eng="act")
            for half in range(2):
                w = wl.get(bl + half)
                wv = V(w.ap.rearrange("p (k n) -> p k n", k=8), w.key)
                for mb in range(2):
                    p = env["psum"].get()
                    for kc in range(8):
                        cx.mm(p, mnT[:, kc, mb * 128:(mb + 1) * 128].k(("mnT", l, kc)), wv[:, kc, :],
                              start=(kc == 0), stop=(kc == 7))
                    cx.copy(Vc[:, mb, half * 512:(half + 1) * 512].k(("Vc", l, mb, half)), p, eng="dve")
            for i in range(4):
                tsl = slice(i * 512, (i + 1) * 512)
                xt = lambda m: xT[:, m, tsl].k(("x", m))
                bs = ws.extend([sq(B_WOUT, m) for m in range(8)] + [sq(B_WQ, m) for m in range(8)] +
                               [sq(B_WO, m) for m in range(8)] +
                               [wv_(l, B_WGU + f * 2048, 2048) for f in range(22)])
                bl = wl.extend([wv_(l, B_WD + m * 2816, 2816) for m in range(8)])
                O2g2 = O2g.ap.rearrange("r (h c) -> (r h) c", h=16)
                for kc in range(8):
                    cx.gather(oTc[kc], O2g2, O2g.key, IDX[("o2", kc, i)])

                def sqmm(base, rhs_fn, evac):
                    for m in range(8):
                        w = ws.get(base + m)
                        wv = V(w.ap[:, 0:1024].rearrange("p (k n) -> p k n", k=8), w.key)
                        p = env["psum"].get()
                        for kc in range(8):
                            cx.mm(p, wv[:, kc, :], rhs_fn(kc), start=(kc == 0), stop=(kc == 7))
                        evac(m, p)

                cast_next(4)
                sqmm(bs, lambda kc: oTc[kc], lambda m, p: cx.tt(xt(m), p, xt(m), ALU.add))
                rmsnorm(cx, env, xt, lambda kc: vcol(l, 2, kc), lambda kc: hT[kc], 512)
                sqmm(bs + 8, lambda kc: hT[kc],
                     lambda m, p: cx.copy(qTc[m], p, eng=("act" if m % 2 else "dve")))
                for hd in range(4):
                    pts = []
                    for mb in range(2):
                        p = env["psum"].get()
                        for dc in range(2):
                            m = hd * 2 + dc
                            cx.mm(p, KcT[:, m, mb * 128:(mb + 1) * 128].k(("KcT", l, m)), qTc[m],
                                  start=(dc == 0), stop=(dc == 1))
                        pt = ptp.get()
                        cx.act(pt, p, AF.Exp, scale=1.0 / 16)
                        pts.append(pt)
                    pl = env["psum"].get()
                    for mb in range(2):
                        cx.mm(pl, env["ones"], pts[mb], start=(mb == 0), stop=(mb == 1))
                    rc = env["f32t"].get()
                    cx.recip(rc, pl)
                    for dc in range(2):
                        m = hd * 2 + dc
                        p = env["psum"].get()
                        for mb in range(2):
                            cx.mm(p, Vc[:, mb, m * 128:(m + 1) * 128].k(("Vc", l, mb, m // 4)), pts[mb],
                                  start=(mb == 0), stop=(mb == 1))
                        cx.tt(ocTc[m], p, rc, ALU.mult)
                sqmm(bs + 16, lambda kc: ocTc[kc], lambda m, p: cx.tt(xt(m), p, xt(m), ALU.add))
                rmsnorm(cx, env, xt, lambda kc: vcol(l, 4, kc), lambda kc: hT[kc], 512)
                ffn(cx, env, xt, hT, actT, ws, wl, bs + 24, bl)
                if last:
                    rstd = rmsnorm(cx, env, xt, lambda kc: vec[:, 82 + kc:83 + kc], lambda kc: hT[kc], 512)
                    for m in range(8):
                        y = env["f32t"].get()
                        cx.stt(y, xt(m), vec[:, 82 + m:83 + m], rstd, ALU.mult, ALU.mult)
                        cx.dma(V(yo.ap.rearrange("(k p) t -> p k t", p=128)[:, m, tsl], ("yo", m, i)), y, final=True)

        def dump_x():
            P.barrier()
            for i in range(4):
                tsl = slice(i * 512, (i + 1) * 512)
                for m in range(8):
                    cx.dma(V(yo.ap.rearrange("(k p) t -> p k t", p=128)[:, m, tsl], ("yo", m, i)),
                           xT[:, m, tsl].k(("x", m)), final=True)

        done = False
        for l in range(2):
            phase_A(l)
            if STOP == "A%d" % l:
                dump_x()
                break
            phase_M(l)
            if STOP == "M%d" % l:
                dump_x()
                break
            phase_B(l, l == 1 and STOP is None)
            if STOP == "B%d" % l:
                dump_x()
                break
            if l == 0:
                P.barrier()
        cx.P.emit(nc, cx.es)
    return nc


_CACHE = {}


def _sqw(w):
    return np.ascontiguousarray(w.reshape(8, 128, 8, 128).transpose(1, 2, 0, 3)).reshape(128, 8192)


def _wgu(wg, wu):
    g = wg.reshape(8, 128, 22, 128).transpose(1, 2, 0, 3)
    u = wu.reshape(8, 128, 22, 128).transpose(1, 2, 0, 3)
    return np.ascontiguousarray(np.stack([g, u], axis=2)).reshape(128, 22 * 2048)


def _wd(wd):
    return np.ascontiguousarray(wd.reshape(22, 128, 8, 128).transpose(1, 2, 0, 3)).reshape(128, 8 * 2816)


def _vec(v):
    return np.ascontiguousarray(v.reshape(8, 128).T)


def kernel(x, mem, ffn1_norm, ffn1_w_gate, ffn1_w_up, ffn1_w_down, mix_norm, w_in, hgrn_lb, hgrn_out_norm,
           fox_f_bias, w_out, cross_norm, mem_norm, cross_wq, cross_wk, cross_wv, cross_wo, ffn2_norm,
           ffn2_w_gate, ffn2_w_up, ffn2_w_down, final_norm):
    f32 = np.float32
    A = lambda a: np.asarray(a, dtype=f32)
    x = A(x)
    mem = A(mem)
    ones = np.ones((128, 128), NPBF)
    ident = np.eye(128, dtype=f32).astype(NPBF)
    jj = np.arange(128)
    maskn = np.where(jj[:, None] > jj[None, :], -30000.0, 0.0).astype(NPBF)
    ss = np.arange(64)
    tri = (ss[:, None] <= ss[None, :]).astype(f32).astype(NPBF)
    scan = np.ones((128, 512), f32)
    scan[:, 0::64] = 0.0
    ones3 = np.ones((3, 8192), NPBF)
    wpk = []
    vcols = []
    for l in range(2):
        win = A(w_in[l])
        cols = []
        for h in range(4):
            for base in (0, 512, 1536):
                cols.append(np.arange(base + h * 128, base + (h + 1) * 128))
        cols.append(np.arange(2048, 3072))
        winf = win[:, np.concatenate(cols)]
        winf = np.ascontiguousarray(winf.reshape(8, 128, 20, 128).transpose(1, 2, 0, 3)).reshape(128, 20 * 1024)
        wint = win[:, np.r_[1024:1536, 3072:3584]]
        wint = np.ascontiguousarray(wint.reshape(8, 128, 2, 512).transpose(1, 2, 0, 3)).reshape(128, 2 * 4096)
        wff = np.ascontiguousarray(win[:, 3584:3592].reshape(8, 128, 8).transpose(1, 0, 2)).reshape(128, 64)
        wv = np.ascontiguousarray(A(cross_wv[l]).reshape(8, 128, 2, 512).transpose(1, 2, 0, 3)).reshape(128, 8192)
        w = np.concatenate([_wgu(A(ffn1_w_gate[l]), A(ffn1_w_up[l])), _wd(A(ffn1_w_down[l])), winf, wint, wff,
                            _sqw(A(w_out[l])), _sqw(A(cross_wq[l])), _sqw(A(cross_wk[l])), _sqw(A(cross_wo[l])), wv,
                            _wgu(A(ffn2_w_gate[l]), A(ffn2_w_up[l])), _wd(A(ffn2_w_down[l]))], axis=1)
        assert w.shape[1] == W_COLS
        wpk.append(np.ascontiguousarray(w))
        vcols += [_vec(A(ffn1_norm[l])), _vec(A(mix_norm[l])), _vec(A(cross_norm[l])), _vec(A(mem_norm[l])),
                  _vec(A(ffn2_norm[l])), A(hgrn_out_norm[l]).reshape(128, 1)]
    vcols.append(_vec(A(final_norm)))
    vecs = np.ascontiguousarray(np.concatenate(vcols, axis=1))
    assert vecs.shape[1] == NVEC
    lb = A(hgrn_lb)
    lbp = np.ascontiguousarray(np.concatenate([lb[0].reshape(4, 128).T, lb[1].reshape(4, 128).T], axis=1))
    fbv = np.ascontiguousarray(A(fox_f_bias).T)
    if "F" not in _CACHE:
        _CACHE["F"] = build_fused()
    in_maps = []
    for c in range(8):
        b, j = c // 4, c % 4
        in_maps.append({"xT": np.ascontiguousarray(x[b, j * 2048:(j + 1) * 2048, :].T),
                        "memT": np.ascontiguousarray(mem[b].T), "w0": wpk[0], "w1": wpk[1], "vecs": vecs,
                        "lbp": lbp, "fb": fbv, "cones": ones, "cident": ident, "cmask": maskn, "ctri": tri,
                        "cscan": scan, "cones3": ones3, "idxd": _idx_table(j)})
    res = run_bass_kernel_spmd(_CACHE["F"], in_maps, core_ids=list(range(8)))
    out = np.zeros((2, 8192, 1024), f32)
    for c in range(8):
        out[c // 4, (c % 4) * 2048:(c % 4 + 1) * 2048, :] = np.asarray(res.results[c]["yo"]).T
    return out
```
